# Optimizing a Trainium2 kernel written in Bass

```python
import jax, jax.numpy as jnp
from jax import lax
import numpy as np

D_MODEL = 4096
BATCH = 4
SEQ = 2048
DEPTH = 1
DEC_BATCH = 16
DEC_SEQ = 16
PAST_LEN = 2048

CHUNK = 64
N_PREV_CHUNKS = 8
N_BAND = N_PREV_CHUNKS + 1
ATT_WINDOW = N_PREV_CHUNKS * CHUNK
D_ATT = D_MODEL // 2
D_POOL = D_MODEL - D_ATT
D_MIX = D_ATT + D_POOL
N_HEADS = 16
HEAD_DIM = D_ATT // N_HEADS
REL_CLIP = 256
POOL_WINDOWS = (2, 4, 8, 16)
N_POOL_GROUPS = len(POOL_WINDOWS)
POOL_GROUP_W = D_POOL // N_POOL_GROUPS
POOL_HIST = max(POOL_WINDOWS) - 1
D_FF = -(-8 * D_MODEL // (3 * 256)) * 256
LN_EPS = 1e-5
DEEPNORM_ALPHA = (2.0 * DEPTH) ** 0.25
DEEPNORM_BETA = (8.0 * DEPTH) ** -0.25
NEG_INF = -1e30

kernel_name = "hybrid_chunk_attn_pool_encoder_step"


def layer_norm(x, g, b):
    xf = x.astype(jnp.float32)
    mu = jnp.mean(xf, axis=-1, keepdims=True)
    var = jnp.mean(jnp.square(xf - mu), axis=-1, keepdims=True)
    y = (xf - mu) * lax.rsqrt(var + LN_EPS) * g.astype(jnp.float32) + b.astype(jnp.float32)
    return y.astype(x.dtype)


def project_in(x, w_in):
    b, t, _ = x.shape
    h = x @ w_in
    q = h[..., :D_ATT].reshape(b, t, N_HEADS, HEAD_DIM)
    k = h[..., D_ATT:2 * D_ATT].reshape(b, t, N_HEADS, HEAD_DIM)
    v = h[..., 2 * D_ATT:3 * D_ATT].reshape(b, t, N_HEADS, HEAD_DIM)
    u = h[..., 3 * D_ATT:]
    return q, k, v, u


def rel_bias_lookup(table, q_offset, tq, tk):
    dist = q_offset + jnp.arange(tq)[:, None] - jnp.arange(tk)[None, :]
    idx = jnp.clip(dist, -REL_CLIP, REL_CLIP) + REL_CLIP
    return table[:, idx]


def band_attention(q, k, v, bias, mask):
    s = jnp.einsum('bcqhd,bckhd->bchqk', q, k).astype(jnp.float32) * (HEAD_DIM ** -0.5)
    s = s + bias.astype(jnp.float32)[None, None]
    s = jnp.where(mask[None, :, None, None, :], s, NEG_INF)
    p = jax.nn.softmax(s, axis=-1).astype(v.dtype)
    return jnp.einsum('bchqk,bckhd->bcqhd', p, v)


def prompt_attention(q, k, v, table):
    b, s = q.shape[:2]
    nc = s // CHUNK
    pad = ((0, 0), (ATT_WINDOW, 0), (0, 0), (0, 0))
    kp = jnp.pad(k, pad).reshape(b, nc + N_PREV_CHUNKS, CHUNK, N_HEADS, HEAD_DIM)
    vp = jnp.pad(v, pad).reshape(b, nc + N_PREV_CHUNKS, CHUNK, N_HEADS, HEAD_DIM)
    kb = jnp.concatenate([kp[:, j:j + nc] for j in range(N_BAND)], axis=2)
    vb = jnp.concatenate([vp[:, j:j + nc] for j in range(N_BAND)], axis=2)
    qc = q.reshape(b, nc, CHUNK, N_HEADS, HEAD_DIM)
    bias = rel_bias_lookup(table, ATT_WINDOW, CHUNK, N_BAND * CHUNK)
    key_pos = (jnp.arange(nc)[:, None] - N_PREV_CHUNKS) * CHUNK + jnp.arange(N_BAND * CHUNK)[None, :]
    mask = key_pos >= 0
    o = band_attention(qc, kb, vb, bias, mask)
    return o.reshape(b, s, D_ATT)


def sample_attention(q, k, v, cache_k, cache_v, table):
    b, t = q.shape[:2]
    lc = cache_k.shape[1]
    kk = jnp.concatenate([cache_k, k], axis=1)[:, None]
    vv = jnp.concatenate([cache_v, v], axis=1)[:, None]
    bias = rel_bias_lookup(table, lc, t, lc + t)
    mask = jnp.ones((1, lc + t), dtype=bool)
    o = band_attention(q[:, None], kk, vv, bias, mask)
    return o[:, 0].reshape(b, t, D_ATT)


def pool_mix(u, hist, hist_len, w_pool, pool_scale):
    b, t, _ = u.shape
    full = jnp.concatenate([hist, u], axis=1).astype(jnp.float32)
    cs = jnp.pad(lax.cumsum(full, axis=1), ((0, 0), (1, 0), (0, 0)))
    pos = jnp.arange(t)
    outs = []
    for g, w in enumerate(POOL_WINDOWS):
        sl = slice(g * POOL_GROUP_W, (g + 1) * POOL_GROUP_W)
        csg = cs[..., sl]
        wsum = csg[:, POOL_HIST + 1:POOL_HIST + 1 + t] - csg[:, POOL_HIST + 1 - w:POOL_HIST + 1 - w + t]
        cnt = jnp.minimum(w, pos + 1 + hist_len).astype(jnp.float32)[None, :, None]
        outs.append(wsum / cnt - full[:, POOL_HIST:, sl])
    d = jnp.stack(outs, axis=2).astype(u.dtype)
    y = jnp.einsum('btgc,gcd->btgd', d, w_pool).reshape(b, t, D_POOL)
    return y * pool_scale


def finish_layer(x, att_o, pool_o, w_out, ln1_g, ln1_b, w_gate, w_up, w_down, ln2_g, ln2_b):
    h = jnp.concatenate([att_o, pool_o], axis=-1) @ w_out
    x1 = layer_norm(DEEPNORM_ALPHA * x + h, ln1_g, ln1_b)
    f = (jax.nn.silu(x1 @ w_gate) * (x1 @ w_up)) @ w_down
    return layer_norm(DEEPNORM_ALPHA * x1 + f, ln2_g, ln2_b)


def setup_inputs(seed: int = 0) -> dict:
    key = jax.random.key(seed)
    ks = jax.random.split(key, 18)
    nrm = jax.random.normal
    lc = min(ATT_WINDOW, PAST_LEN)
    return {
        'x_prompt': nrm(ks[0], (BATCH, SEQ, D_MODEL), jnp.float32),
        'x_sample': nrm(ks[1], (DEC_BATCH, DEC_SEQ, D_MODEL), jnp.float32),
        'cache_k': nrm(ks[2], (DEPTH, DEC_BATCH, lc, N_HEADS, HEAD_DIM), jnp.float32),
        'cache_v': nrm(ks[3], (DEPTH, DEC_BATCH, lc, N_HEADS, HEAD_DIM), jnp.float32),
        'state_pool': nrm(ks[4], (DEPTH, DEC_BATCH, POOL_HIST, D_POOL), jnp.float32),
        'w_in': nrm(ks[5], (DEPTH, D_MODEL, 3 * D_ATT + D_POOL), jnp.float32) * D_MODEL ** -0.5,
        'rel_bias': 0.5 * nrm(ks[6], (DEPTH, N_HEADS, 2 * REL_CLIP + 1), jnp.float32),
        'w_pool': nrm(ks[7], (DEPTH, N_POOL_GROUPS, POOL_GROUP_W, POOL_GROUP_W), jnp.float32) * POOL_GROUP_W ** -0.5,
        'pool_scale': 1.0 + 0.1 * nrm(ks[8], (DEPTH, D_POOL), jnp.float32),
        'w_out': nrm(ks[9], (DEPTH, D_MIX, D_MODEL), jnp.float32) * (D_MIX ** -0.5 * DEEPNORM_BETA),
        'ln1_g': 1.0 + 0.1 * nrm(ks[10], (DEPTH, D_MODEL), jnp.float32),
        'ln1_b': 0.01 * nrm(ks[11], (DEPTH, D_MODEL), jnp.float32),
        'w_gate': nrm(ks[12], (DEPTH, D_MODEL, D_FF), jnp.float32) * D_MODEL ** -0.5,
        'w_up': nrm(ks[13], (DEPTH, D_MODEL, D_FF), jnp.float32) * D_MODEL ** -0.5,
        'w_down': nrm(ks[14], (DEPTH, D_FF, D_MODEL), jnp.float32) * (D_FF ** -0.5 * DEEPNORM_BETA),
        'ln2_g': 1.0 + 0.1 * nrm(ks[15], (DEPTH, D_MODEL), jnp.float32),
        'ln2_b': 0.01 * nrm(ks[16], (DEPTH, D_MODEL), jnp.float32),
    }


def reference(x_prompt, x_sample, cache_k, cache_v, state_pool, w_in, rel_bias, w_pool, pool_scale,
              w_out, ln1_g, ln1_b, w_gate, w_up, w_down, ln2_g, ln2_b):
    yp, ys = x_prompt, x_sample
    kp_new, vp_new, pp_new, ks_new, vs_new, ps_new = [], [], [], [], [], []
    keep_p = min(ATT_WINDOW, x_prompt.shape[1])
    for l in range(DEPTH):
        qp, kp, vp, up = project_in(yp, w_in[l])
        att_p = prompt_attention(qp, kp, vp, rel_bias[l])
        hist0 = jnp.zeros((up.shape[0], POOL_HIST, D_POOL), up.dtype)
        pool_p = pool_mix(up, hist0, 0, w_pool[l], pool_scale[l])
        qs, k_s, v_s, us = project_in(ys, w_in[l])
        att_s = sample_attention(qs, k_s, v_s, cache_k[l], cache_v[l], rel_bias[l])
        pool_s = pool_mix(us, state_pool[l], POOL_HIST, w_pool[l], pool_scale[l])

        kp_new.append(kp[:, -keep_p:])
        vp_new.append(vp[:, -keep_p:])
        pp_new.append(up[:, -POOL_HIST:])
        ks_new.append(k_s)
        vs_new.append(v_s)
        ps_new.append(jnp.concatenate([state_pool[l], us], axis=1)[:, -POOL_HIST:])

        yp = finish_layer(yp, att_p, pool_p, w_out[l], ln1_g[l], ln1_b[l], w_gate[l], w_up[l], w_down[l], ln2_g[l], ln2_b[l])
        ys = finish_layer(ys, att_s, pool_s, w_out[l], ln1_g[l], ln1_b[l], w_gate[l], w_up[l], w_down[l], ln2_g[l], ln2_b[l])
    return (yp, ys, jnp.stack(kp_new), jnp.stack(vp_new), jnp.stack(pp_new),
            jnp.stack(ks_new), jnp.stack(vs_new), jnp.stack(ps_new))
```

```python
from contextlib import ExitStack

import numpy as np
import concourse.bass as bass
import concourse.mybir as mybir
from concourse.bass_utils import run_bass_kernel_spmd

F32 = mybir.dt.float32
BF16 = mybir.dt.bfloat16
ALU = mybir.AluOpType
AF = mybir.ActivationFunctionType
AX = mybir.AxisListType

NCORES = 8
D = 4096
DFF = 11008
KT = D // 128
KTF = DFF // 128
NH = 512
NP = 1024
NS = 32
NM = NP + NS
NALL = NH + NM
NKO = 544
NEG = -30000.0
ALPHA = 2.0 ** 0.25
SCALE = 128.0 ** -0.5
EPS = 1e-5
SLOT = 8192
NSLOT = 3
import os
KSTOP = int(os.environ.get("KSTOP", "0"))


class Q:
    def __init__(self, name, eng, sem, is_pe=False):
        self.name, self.eng, self.sem, self.count, self.waited, self.is_pe = name, eng, sem, 0, {}, is_pe


class DSem:
    def __init__(self, sem):
        self.sem, self.count = sem, 0


class Res:
    def __init__(self, name, dsem=None):
        self.name, self.w, self.r, self.d = name, {}, {}, dsem
        self.excl = name.startswith("psum:")


def _merge(dst, tok):
    k = id(tok[0])
    if k not in dst or dst[k][1] < tok[1]:
        dst[k] = tok


class Tracker:
    def __init__(self, nc, stack):
        self.nc = nc
        mk = lambda n: stack.enter_context(nc.semaphore(n))
        self.PE = Q("pe", nc.tensor, mk("s_pe"), True)
        self.ACT = Q("act", nc.scalar, mk("s_act"))
        self.DVE = Q("dve", nc.vector, mk("s_dve"))
        self.POOL = Q("pool", nc.gpsimd, mk("s_pool"))
        self.SP = Q("sp", nc.sync, mk("s_sp"))
        self.queues = [self.PE, self.ACT, self.DVE, self.POOL, self.SP]
        self.free_hw = [DSem(mk(f"s_h{i}")) for i in range(30)]
        self.free_sw = [DSem(mk(f"s_w{i}")) for i in range(10)]
        self.all_dsems = self.free_hw + self.free_sw

    def res(self, name, dma=False):
        if not dma:
            return Res(name, None)
        r = Res(name, (self.free_sw if dma == "sw" else self.free_hw).pop())
        r.kind = "sw" if dma == "sw" else "hw"
        return r

    def release(self, *rs):
        for r in rs:
            if r.d is not None:
                (self.free_sw if r.kind == "sw" else self.free_hw).append(r.d)
                r.d = None

    def _wait(self, q, toks):
        for sem, v in toks:
            k = id(sem)
            if q.waited.get(k, 0) >= v:
                continue
            if sem is q.sem and q.is_pe:
                continue
            q.eng.wait_ge(sem, v)
            q.waited[k] = v

    def _deps(self, reads, writes, disjoint):
        deps = []
        for r in reads:
            deps += list(r.w.values())
            if r.excl:
                deps += list(r.r.values())
        for w in writes:
            deps += list(w.r.values())
            if not disjoint:
                deps += list(w.w.values())
        return deps

    def _commit(self, tok, reads, writes, disjoint):
        for r in reads:
            _merge(r.r, tok)
        for w in writes:
            if disjoint:
                _merge(w.w, tok)
            else:
                for t in w.w.values():
                    _merge(w.r, t)
                w.w = {id(tok[0]): tok}

    def epoch(self, *rs):
        for r in rs:
            for t in r.w.values():
                _merge(r.r, t)
            r.w = {}

    def op(self, q, fn, reads=(), writes=(), disjoint=False):
        self._wait(q, self._deps(reads, writes, disjoint))
        ins = fn(q.eng)
        q.count += 1
        ins.then_inc(q.sem, 1)
        self._commit((q.sem, q.count), reads, writes, disjoint)

    def group(self, fns, reads=(), writes=(), disjoint=False):
        q = self.PE
        self._wait(q, self._deps(reads, writes, disjoint))
        ins = None
        for fn in fns:
            ins = fn(q.eng)
        q.count += 1
        ins.then_inc(q.sem, 1)
        self._commit((q.sem, q.count), reads, writes, disjoint)

    def dma(self, q, out, in_, owner, reads=(), writes=(), disjoint=False):
        assert (owner.kind == "sw") == (q is self.POOL), owner.name
        self._wait(q, self._deps(reads, writes, disjoint))
        ins = q.eng.dma_start(out=out, in_=in_)
        owner.d.count += 16
        ins.then_inc(owner.d.sem, 16)
        self._commit((owner.d.sem, owner.d.count), reads, writes, disjoint)

    def barrier(self):
        toks = [(q.sem, q.count) for q in self.queues if q.count > 0]
        toks += [(d.sem, d.count) for d in self.all_dsems if d.count > 0]
        for q in self.queues:
            self._wait(q, toks)

    def final(self):
        toks = [(q.sem, q.count) for q in self.queues if q.count > 0]
        toks += [(d.sem, d.count) for d in self.all_dsems if d.count > 0]
        self._wait(self.SP, toks)


def build_nc():
    nc = bass.Bass("TRN2", target_bir_lowering=False)

    def din(name, shape, dt=F32):
        return nc.dram_tensor(name, list(shape), dt, kind="ExternalInput").ap()

    def dout(name, shape, dt=F32):
        return nc.dram_tensor(name, list(shape), dt, kind="ExternalOutput").ap()

    def dscr(name, shape, dt=F32):
        return nc.dram_tensor(name, list(shape), dt, kind="Internal").ap()

    xc = din("xc", [NALL, D])
    ck = din("ck", [2, 512, 2048])
    cv = din("cv", [2, 512, 2048])
    spd = din("sp", [2, 15, 2048])
    w_in = din("w_in", [D, 8192])
    w_out = din("w_out", [D, D])
    w_gate = din("w_gate", [D, DFF])
    w_up = din("w_up", [D, DFF])
    w_down = din("w_down", [DFF, D])
    w_pool = din("w_pool", [4, 512, 512])
    pscale = din("pscale", [128, 16])
    g1 = din("g1", [128, D])
    b1 = din("b1", [128, D])
    g2 = din("g2", [128, D])
    b2 = din("b2", [128, D])
    biasT = din("biasT", [16, 128, 640])
    maskT = din("maskT", [128, 640])
    halo = din("halo", [128, 1])
    invcnt = din("invcnt", [128, 64])

    y_o = dout("y", [NM, D])
    k_o = dout("k_out", [NKO, 2048])
    v_o = dout("v_out", [NKO, 2048])
    u_o = dout("u_out", [48, 2048])

    mix_scr = dscr("mix_scr", [32, 128, NM], BF16)
    z1_scr = dscr("z1_scr", [NM, D])
    x1_scr = dscr("x1_scr", [NM, D])
    hT_scr = dscr("hT_scr", [KTF, 128, NM], BF16)
    z2_scr = dscr("z2_scr", [NM, D])

    w_in_v = w_in.rearrange("(k p) n -> p k n", p=128)
    w_out_v = w_out.rearrange("(k p) n -> p k n", p=128)
    w_gate_v = w_gate.rearrange("(k p) n -> p k n", p=128)
    w_up_v = w_up.rearrange("(k p) n -> p k n", p=128)
    w_down_v = w_down.rearrange("(k p) n -> p k n", p=128)

    slabs = []
    for h in range(16):
        slabs.append((KT, 256, [(0, w_in_v[:, :, h * 128:(h + 1) * 128]),
                                (128, w_in_v[:, :, 2048 + h * 128:2048 + (h + 1) * 128])]))
        slabs.append((KT, 128, [(0, w_in_v[:, :, 4096 + h * 128:4096 + (h + 1) * 128])]))
    for g in range(4):
        for i in range(2):
            c0 = 6144 + (g * 2 + i) * 256
            slabs.append((KT, 256, [(0, w_in_v[:, :, c0:c0 + 256])]))
        slabs.append((4, 512, [(0, w_pool[g].rearrange("(k p) n -> p k n", p=128))]))
    for i in range(16):
        slabs.append((KT, 256, [(0, w_out_v[:, :, i * 256:(i + 1) * 256])]))
    for j in range(KTF):
        slabs.append((KT, 256, [(0, w_gate_v[:, :, j * 128:(j + 1) * 128]),
                                (128, w_up_v[:, :, j * 128:(j + 1) * 128])]))
    for hf in range(2):
        for m in range(32):
            slabs.append((43, 128, [(0, w_down_v[:, 0:43, m * 128:(m + 1) * 128])]))
            slabs.append((43, 128, [(0, w_down_v[:, 43:86, m * 128:(m + 1) * 128])]))

    with ExitStack() as G:
        T = Tracker(nc, G)
        PE, ACT, DVE, POOL, SP = T.PE, T.ACT, T.DVE, T.POOL, T.SP

        uid = {"n": 0}

        def sb(stack, name, shape, dt):
            uid["n"] += 1
            return stack.enter_context(nc.sbuf_tensor(f"{name}_{uid['n']}", list(shape), dt))

        def ps(stack, name, shape, dt=F32):
            uid["n"] += 1
            return stack.enter_context(nc.psum_tensor(f"{name}_{uid['n']}", list(shape), dt))

        ring = [sb(G, f"ring{i}", [128, SLOT], BF16) for i in range(NSLOT)]
        R_ring = [T.res(f"ring{i}", dma="sw") for i in range(NSLOT)]
        ident_f = sb(G, "ident_f", [128, 128], F32)
        ident_b = sb(G, "ident_b", [128, 128], BF16)
        ones_b = sb(G, "ones_b", [128, 128], BF16)
        R_const = T.res("const", dma=True)

        ident_in = din("ident", [128, 128])
        T.dma(SP, ident_f[:, :], ident_in[:, :], R_const, writes=[R_const])
        T.op(DVE, lambda e: e.tensor_copy(out=ident_b[:, :], in_=ident_f[:, :]), reads=[R_const], writes=[R_const])
        T.op(DVE, lambda e: e.memset(ones_b[:, :], 1.0), writes=[R_const], disjoint=True)

        class WS:
            def __init__(self):
                self.issued = 0
                self.next_i = 0

            def _issue(self, i):
                kt, ncols, pieces = slabs[i]
                s = i % NSLOT
                view = ring[s][:, 0:kt * ncols].rearrange("p (k n) -> p k n", n=ncols)
                first = True
                for off, src in pieces:
                    pc = src.shape[-1]
                    T.dma(POOL, view[:, :, off:off + pc], src, R_ring[s], writes=[R_ring[s]], disjoint=not first)
                    first = False

            def prefetch(self, upto):
                while self.issued < min(upto, len(slabs)):
                    self._issue(self.issued)
                    self.issued += 1

            def next(self):
                i = self.next_i
                self.next_i += 1
                self.prefetch(i + NSLOT)
                kt, ncols, _ = slabs[i]
                s = i % NSLOT
                return ring[s][:, 0:kt * ncols].rearrange("p (k n) -> p k n", n=ncols), R_ring[s]

        ws = WS()

        state = {"bank": 0, "alt": 0}

        def proj(segs, mt, inT, R_in, chunks, evac, banks, R_banks, in_off=0):
            nb = len(banks)
            bs = []
            for _ in chunks:
                bs.append(state["bank"] % nb)
                state["bank"] += 1
            ktot = sum(kt for _, kt in segs)
            kk0 = 0
            for si, (getter, kt) in enumerate(segs):
                slab, R_slab = getter()
                for ci, (c0, n) in enumerate(chunks):
                    b = bs[ci]
                    fns = []
                    for k in range(kt):
                        kk = kk0 + k
                        fns.append(lambda e, k=k, kk=kk: e.matmul(
                            banks[b][:, 0:n], slab[:, k, mt * 128:(mt + 1) * 128],
                            inT[:, kk, in_off + c0:in_off + c0 + n], start=(kk == 0), stop=(kk == ktot - 1)))
                    T.group(fns, reads=[R_slab, R_in[si] if isinstance(R_in, list) else R_in],
                            writes=[R_banks[b]], disjoint=(si > 0))
                    if si == len(segs) - 1:
                        evac(ci, banks[b], R_banks[b], c0, n)
                kk0 += kt

        def cg(t):
            return lambda: t

        CH_ALL = [(392 * i, 392) for i in range(4)]
        CH_MAIN = [(352 * i, 352) for i in range(3)]
        CH_U = [(496 + 268 * i, 268) for i in range(4)]

        with ExitStack() as SA_:
            xT = sb(SA_, "xT", [128, KT, NALL], BF16)
            R_xT = T.res("xT")
            with ExitStack() as S1:
                xb = [sb(S1, f"xb{i}", [128, D], BF16) for i in range(4)]
                R_xb = [T.res(f"xb{i}", dma="sw") for i in range(4)]
                tpb = [ps(S1, f"tpb{i}", [128, 1024], BF16) for i in range(2)]
                R_tpb = [T.res(f"psum:tpb{i}") for i in range(2)]
                ws.prefetch(NSLOT)
                cnt = 0
                for r in range(13):
                    rows = 128 if r < 12 else 32
                    r0 = r * 128
                    T.dma(POOL, xb[r % 4][0:rows, :], xc[r0:r0 + rows, :], R_xb[r % 4], writes=[R_xb[r % 4]])
                    for g4 in range(4):
                        b = cnt % 2
                        cnt += 1
                        fns = []
                        for j in range(8):
                            k = g4 * 8 + j
                            fns.append(lambda e, j=j, k=k, b=b: e.transpose(
                                tpb[b][:, j * 128:j * 128 + rows], xb[r % 4][0:rows, k * 128:(k + 1) * 128],
                                ident_b[0:rows, 0:rows]))
                        T.group(fns, reads=[R_xb[r % 4], R_const], writes=[R_tpb[b]])
                        src = tpb[b][:, :].rearrange("p (j n) -> p j n", n=128)[:, :, 0:rows]
                        dst = xT[:, g4 * 8:(g4 + 1) * 8, r0:r0 + rows]
                        if cnt % 2 == 0:
                            T.op(DVE, lambda e, s=src, d=dst: e.tensor_copy(out=d, in_=s),
                                 reads=[R_tpb[b]], writes=[R_xT], disjoint=True)
                        else:
                            T.op(ACT, lambda e, s=src, d=dst: e.copy(out=d, in_=s),
                                 reads=[R_tpb[b]], writes=[R_xT], disjoint=True)
                T.barrier()
                if KSTOP == 1:
                    T.final()
                    return nc
                T.release(*R_xb)

            with ExitStack() as S2:
                pb = [ps(S2, f"pj{i}", [128, 512]) for i in range(2)]
                R_pb = [T.res(f"psum:pj{i}") for i in range(2)]
                tpf = [ps(S2, f"tpf{i}", [128, 512]) for i in range(2)]
                R_tpf = [T.res(f"psum:tpf{i}") for i in range(2)]
                SAb = ps(S2, "SA", [128, 512])
                SBb = ps(S2, "SB", [128, 512])
                Obs = [ps(S2, f"O{i}", [128, 512]) for i in range(2)]
                R_Os = [T.res(f"psum:O{i}") for i in range(2)]
                R_SA, R_SB = T.res("psum:SA"), T.res("psum:SB")
                QT = sb(S2, "QT", [128, NM], BF16)
                KTb = sb(S2, "KT", [128, NALL], BF16)
                KTf = sb(S2, "KTf", [128, NKO], F32)
                VTf = sb(S2, "VTf", [128, NALL], F32)
                Vb = sb(S2, "Vb", [128, 13, 128], BF16)
                Vs = sb(S2, "Vs", [16, 2, 128], BF16)
                kst = [sb(S2, f"kst{i}", [128, 5, 128], F32) for i in range(2)]
                vst = [sb(S2, f"vst{i}", [128, 5, 128], F32) for i in range(2)]
                atT = [sb(S2, f"atT{i}", [128, NM], BF16) for i in range(2)]
                braw = sb(S2, "braw", [128, 640], F32)
                bm = sb(S2, "bm", [128, 640], F32)
                bh = sb(S2, "bh", [128, 640], F32)
                mk = sb(S2, "mk", [128, 640], F32)
                hl = sb(S2, "hl", [128, 1], F32)
                tmp = [sb(S2, f"tmp{i}", [128, 640], F32) for i in range(2)]
                PT = [sb(S2, f"PT{i}", [128, 640], BF16) for i in range(2)]
                rsbs = [sb(S2, f"rsb{i}", [128, 128], F32) for i in range(2)]
                ckfs = [sb(S2, f"ckf{i}", [128, 4, 128], F32) for i in range(2)]
                ckTs = [sb(S2, f"ckT{i}", [128, 512], BF16) for i in range(2)]
                cvbs = [sb(S2, f"cvb{i}", [128, 4, 128], BF16) for i in range(2)]
                tmpss = [sb(S2, f"tmps{i}", [128, 5, 16], F32) for i in range(2)]
                PTss = [sb(S2, f"PTs{i}", [128, 5, 16], BF16) for i in range(2)]
                R_QT, R_KT, R_KTf, R_VTf, R_Vb, R_Vs = (T.res(n) for n in ("QT", "KT", "KTf", "VTf", "Vb", "Vs"))
                R_kst = [T.res(f"kst{i}", dma=True) for i in range(2)]
                R_vst = [T.res(f"vst{i}", dma=True) for i in range(2)]
                R_atT = [T.res(f"atT{i}", dma=True) for i in range(2)]
                R_braw = T.res("braw", dma=True)
                R_bm, R_bh = T.res("bm"), T.res("bh")
                R_mk = T.res("mk", dma=True)
                R_tmp = [T.res(f"tmp{i}") for i in range(2)]
                R_PT = [T.res(f"PT{i}") for i in range(2)]
                R_rsbs = [T.res(f"rsb{i}") for i in range(2)]
                R_ckfs = [T.res(f"ckf{i}", dma=True) for i in range(2)]
                R_ckTs = [T.res(f"ckT{i}") for i in range(2)]
                R_cvbs = [T.res(f"cvb{i}", dma="sw") for i in range(2)]
                R_tmpss = [T.res(f"tmps{i}") for i in range(2)]
                R_PTss = [T.res(f"PTs{i}") for i in range(2)]

                T.dma(SP, mk[:, :], maskT[:, :], R_mk, writes=[R_mk])
                T.dma(SP, hl[:, :], halo[:, :], R_mk, writes=[R_mk], disjoint=True)

                k_o_main = k_o[0:512, :].rearrange("(t p) c -> p t c", p=128)
                v_o_main = v_o[0:512, :].rearrange("(t p) c -> p t c", p=128)

                for h in range(16):
                    hp = h % 2
                    tA = ws.next()
                    T.epoch(R_QT, R_KT, R_KTf, R_VTf, R_Vb, R_Vs)
                    for s_ in range(2):
                        T.dma(SP, ckfs[s_][:, :, :],
                              ck[s_, :, h * 128:(h + 1) * 128].rearrange("(t p) d -> p t d", p=128),
                              R_ckfs[s_], writes=[R_ckfs[s_]])
                        T.dma(POOL, cvbs[s_][:, :, :],
                              cv[s_, :, h * 128:(h + 1) * 128].rearrange("(t p) d -> p t d", p=128),
                              R_cvbs[s_], writes=[R_cvbs[s_]])
                    T.dma(SP, braw[:, :], biasT[h], R_braw, writes=[R_braw])
                    T.op(DVE, lambda e: e.tensor_tensor(out=bm[:, :], in0=braw[:, :], in1=mk[:, :], op=ALU.add),
                         reads=[R_braw, R_mk], writes=[R_bm])
                    T.op(DVE, lambda e: e.tensor_scalar(out=bh[:, 0:512], in0=bm[:, 0:512], scalar1=hl[:, 0:1],
                                                        scalar2=None, op0=ALU.add),
                         reads=[R_bm, R_mk], writes=[R_bh])

                    def evac_q(ci, bank, R_bank, c0, n):
                        T.op(ACT, lambda e: e.copy(out=QT[:, c0:c0 + n], in_=bank[:, 0:n]),
                             reads=[R_bank], writes=[R_QT], disjoint=True)
                    proj([(cg(tA), KT)], 0, xT, R_xT, CH_MAIN, evac_q, pb, R_pb, in_off=NH)

                    def evac_k(ci, bank, R_bank, c0, n):
                        T.op(DVE, lambda e: e.tensor_copy(out=KTb[:, c0:c0 + n], in_=bank[:, 0:n]),
                             reads=[R_bank], writes=[R_KT], disjoint=True)
                        if c0 + n > 1024:
                            lo = max(c0, 1024)
                            T.op(ACT, lambda e: e.copy(out=KTf[:, lo - 1024:c0 + n - 1024], in_=bank[:, lo - c0:n]),
                                 reads=[R_bank], writes=[R_KTf], disjoint=True)
                    proj([(cg(tA), KT)], 1, xT, R_xT, CH_ALL, evac_k, pb, R_pb)
                    def k_out_tr():
                        tb = state["alt"] % 2
                        state["alt"] += 1
                        fns = []
                        for i in range(5):
                            w = 128 if i < 4 else 32
                            fns.append(lambda e, i=i, w=w, tb=tb: e.transpose(
                                tpf[tb][0:w, (i % 4) * 128:(i % 4) * 128 + 128], KTf[:, i * 128:i * 128 + w], ident_f[:, :]))
                        T.group(fns[0:4], reads=[R_KTf, R_const], writes=[R_tpf[tb]])
                        T.op(ACT, lambda e, tb=tb: e.copy(out=kst[hp][:, 0:4, :],
                                                          in_=tpf[tb][:, :].rearrange("p (j n) -> p j n", n=128)),
                             reads=[R_tpf[tb]], writes=[R_kst[hp]])
                        tb2 = state["alt"] % 2
                        state["alt"] += 1
                        fns5 = [lambda e, tb2=tb2: e.transpose(tpf[tb2][0:32, 0:128], KTf[:, 512:544], ident_f[:, :])]
                        T.group(fns5, reads=[R_KTf, R_const], writes=[R_tpf[tb2]])
                        T.op(ACT, lambda e, tb2=tb2: e.copy(out=kst[hp][0:32, 4, :], in_=tpf[tb2][0:32, 0:128]),
                             reads=[R_tpf[tb2]], writes=[R_kst[hp]], disjoint=True)
                        T.dma(SP, k_o_main[:, :, h * 128:(h + 1) * 128], kst[hp][:, 0:4, :], R_kst[hp], reads=[R_kst[hp]])
                        T.dma(SP, k_o[512:544, h * 128:(h + 1) * 128], kst[hp][0:32, 4, :], R_kst[hp], reads=[R_kst[hp]])

                    def evac_v(ci, bank, R_bank, c0, n):
                        T.op(ACT, lambda e: e.copy(out=VTf[:, c0:c0 + n], in_=bank[:, 0:n]),
                             reads=[R_bank], writes=[R_VTf], disjoint=True)
                    proj([(ws.next, KT)], 0, xT, R_xT, CH_ALL, evac_v, pb, R_pb)
                    k_out_tr()
                    for s_ in range(2):
                        tb = state["alt"] % 2
                        state["alt"] += 1
                        T.group([lambda e, j=j, tb=tb: e.transpose(tpf[tb][:, j * 128:(j + 1) * 128], ckfs[s_][:, j, :],
                                                                 ident_f[:, :]) for j in range(4)],
                                reads=[R_ckfs[s_], R_const], writes=[R_tpf[tb]])
                        T.op(ACT, lambda e, tb=tb: e.copy(out=ckTs[s_][:, :], in_=tpf[tb][:, :]),
                             reads=[R_tpf[tb]], writes=[R_ckTs[s_]])

                    for g4 in range(4):
                        tiles = list(range(g4 * 4, min(g4 * 4 + 4, 13)))
                        tb = state["alt"] % 2
                        state["alt"] += 1
                        fns = []
                        for j, r in enumerate(tiles):
                            w = 128 if r < 12 else 32
                            fns.append(lambda e, j=j, r=r, w=w, tb=tb: e.transpose(
                                tpf[tb][0:w, j * 128:(j + 1) * 128], VTf[:, r * 128:r * 128 + w], ident_f[:, :]))
                        T.group(fns, reads=[R_VTf, R_const], writes=[R_tpf[tb]])
                        if g4 < 3:
                            T.op(DVE, lambda e, tb=tb, g4=g4: e.tensor_copy(
                                out=Vb[:, g4 * 4:g4 * 4 + 4, :], in_=tpf[tb][:, :].rearrange("p (j n) -> p j n", n=128)),
                                reads=[R_tpf[tb]], writes=[R_Vb], disjoint=True)
                            if g4 == 2:
                                T.op(ACT, lambda e, tb=tb: e.copy(
                                    out=vst[hp][:, 0:4, :], in_=tpf[tb][:, :].rearrange("p (j n) -> p j n", n=128)),
                                    reads=[R_tpf[tb]], writes=[R_vst[hp]])
                        else:
                            T.op(DVE, lambda e, tb=tb: e.tensor_copy(out=Vb[0:32, 12, :], in_=tpf[tb][0:32, 0:128]),
                                 reads=[R_tpf[tb]], writes=[R_Vb], disjoint=True)
                            T.op(ACT, lambda e, tb=tb: e.copy(out=vst[hp][0:32, 4, :], in_=tpf[tb][0:32, 0:128]),
                                 reads=[R_tpf[tb]], writes=[R_vst[hp]], disjoint=True)
                    T.dma(SP, v_o_main[:, :, h * 128:(h + 1) * 128], vst[hp][:, 0:4, :], R_vst[hp], reads=[R_vst[hp]])
                    T.dma(SP, v_o[512:544, h * 128:(h + 1) * 128], vst[hp][0:32, 4, :], R_vst[hp], reads=[R_vst[hp]])
                    tb = state["alt"] % 2
                    state["alt"] += 1
                    T.group([lambda e, s=s, tb=tb: e.transpose(tpf[tb][0:16, s * 128:(s + 1) * 128],
                                                             VTf[:, 1536 + 16 * s:1552 + 16 * s], ident_f[:, :])
                             for s in range(2)], reads=[R_VTf, R_const], writes=[R_tpf[tb]])
                    T.op(DVE, lambda e, tb=tb: e.tensor_copy(
                        out=Vs[:, :, :], in_=tpf[tb][0:16, 0:256].rearrange("p (j n) -> p j n", n=128)),
                        reads=[R_tpf[tb]], writes=[R_Vs])

                    def att_S(t):
                        q_ap = QT[:, t * 128:(t + 1) * 128]
                        T.group([lambda e, kb=kb: e.matmul(SAb[:, kb * 128:(kb + 1) * 128],
                                                           KTb[:, (t + kb) * 128:(t + kb + 1) * 128], q_ap,
                                                           start=True, stop=True) for kb in range(4)],
                                reads=[R_KT, R_QT], writes=[R_SA])
                        T.group([lambda e: e.matmul(SBb[:, 0:128], KTb[:, (t + 4) * 128:(t + 5) * 128], q_ap,
                                                    start=True, stop=True)],
                                reads=[R_KT, R_QT], writes=[R_SB])

                    def att_soft(t):
                        tp_ = t % 2
                        nh = max(0, 4 - t)
                        if nh > 0:
                            T.op(DVE, lambda e: e.scalar_tensor_tensor(
                                out=tmp[tp_][:, 0:nh * 128], in0=SAb[:, 0:nh * 128], scalar=SCALE,
                                in1=bh[:, 0:nh * 128], op0=ALU.mult, op1=ALU.add),
                                reads=[R_SA, R_bh], writes=[R_tmp[tp_]])
                        if nh < 4:
                            T.op(DVE, lambda e: e.scalar_tensor_tensor(
                                out=tmp[tp_][:, nh * 128:512], in0=SAb[:, nh * 128:512], scalar=SCALE,
                                in1=bm[:, nh * 128:512], op0=ALU.mult, op1=ALU.add),
                                reads=[R_SA, R_bm], writes=[R_tmp[tp_]], disjoint=(nh > 0))
                        T.op(DVE, lambda e: e.scalar_tensor_tensor(
                            out=tmp[tp_][:, 512:640], in0=SBb[:, 0:128], scalar=SCALE,
                            in1=bm[:, 512:640], op0=ALU.mult, op1=ALU.add),
                            reads=[R_SB, R_bm], writes=[R_tmp[tp_]], disjoint=True)
                        T.op(ACT, lambda e: e.activation(out=PT[tp_][:, :], in_=tmp[tp_][:, :], func=AF.Exp),
                             reads=[R_tmp[tp_]], writes=[R_PT[tp_]])

                    def att_PV(t):
                        tp_ = t % 2
                        Ob, R_O, rsb, R_rsb = Obs[tp_], R_Os[tp_], rsbs[tp_], R_rsbs[tp_]
                        T.group([lambda e, kb=kb: e.matmul(Ob[:, 0:128], Vb[:, t + kb, :],
                                                           PT[tp_][:, kb * 128:(kb + 1) * 128],
                                                           start=(kb == 0), stop=(kb == 4)) for kb in range(5)] +
                                [lambda e, kb=kb: e.matmul(Ob[:, 128:256], ones_b[:, :],
                                                           PT[tp_][:, kb * 128:(kb + 1) * 128],
                                                           start=(kb == 0), stop=(kb == 4), skip_group_check=True)
                                 for kb in range(5)],
                                reads=[R_Vb, R_PT[tp_], R_const], writes=[R_O])
                        T.op(DVE, lambda e: e.reciprocal(out=rsb[:, :], in_=Ob[:, 128:256]),
                             reads=[R_O], writes=[R_rsb])
                        T.op(DVE, lambda e: e.tensor_tensor(out=atT[hp][:, t * 128:(t + 1) * 128], in0=Ob[:, 0:128],
                                                            in1=rsb[:, :], op=ALU.mult),
                             reads=[R_O, R_rsb], writes=[R_atT[hp]], disjoint=(t > 0))

                    att_S(0)
                    att_soft(0)
                    for t in range(8):
                        if t + 1 < 8:
                            att_S(t + 1)
                            att_soft(t + 1)
                        att_PV(t)

                    bm3 = bm[:, :].rearrange("p (b n) -> p b n", n=128)

                    def sm_S(s_):
                        qs = QT[:, 1024 + 16 * s_:1040 + 16 * s_]
                        o = s_ * 128
                        T.group([lambda e, kb=kb: e.matmul(SAb[:, o + kb * 16:o + (kb + 1) * 16],
                                                           ckTs[s_][:, kb * 128:(kb + 1) * 128], qs,
                                                           start=True, stop=True) for kb in range(4)] +
                                [lambda e: e.matmul(SAb[0:16, o + 64:o + 80], KTb[:, 1536 + 16 * s_:1552 + 16 * s_], qs,
                                                    start=True, stop=True, skip_group_check=True)],
                                reads=[R_ckTs[s_], R_QT, R_KT], writes=[R_SA])
                        tmps, R_tmps, PTs, R_PTs = tmpss[s_], R_tmpss[s_], PTss[s_], R_PTss[s_]
                        T.op(DVE, lambda e: e.scalar_tensor_tensor(
                            out=tmps[:, 0:4, :], in0=SAb[:, o:o + 64].rearrange("p (b n) -> p b n", n=16), scalar=SCALE,
                            in1=bm3[:, 0:4, 0:16], op0=ALU.mult, op1=ALU.add),
                            reads=[R_SA, R_bm], writes=[R_tmps])
                        T.op(DVE, lambda e: e.scalar_tensor_tensor(
                            out=tmps[0:16, 4, :], in0=SAb[0:16, o + 64:o + 80], scalar=SCALE,
                            in1=bm[0:16, 512:528], op0=ALU.mult, op1=ALU.add),
                            reads=[R_SA, R_bm], writes=[R_tmps], disjoint=True)
                        T.op(ACT, lambda e: e.activation(out=PTs[:, 0:4, :], in_=tmps[:, 0:4, :], func=AF.Exp),
                             reads=[R_tmps], writes=[R_PTs])
                        T.op(ACT, lambda e: e.activation(out=PTs[0:16, 4, :], in_=tmps[0:16, 4, :], func=AF.Exp),
                             reads=[R_tmps], writes=[R_PTs], disjoint=True)

                    def sm_PV(s_):
                        Ob, R_O, rsb, R_rsb = Obs[s_], R_Os[s_], rsbs[s_], R_rsbs[s_]
                        PTs, R_PTs, cvb, R_cvb = PTss[s_], R_PTss[s_], cvbs[s_], R_cvbs[s_]
                        T.group([lambda e, kb=kb: e.matmul(Ob[:, 0:16], cvb[:, kb, :], PTs[:, kb, :],
                                                           start=(kb == 0), stop=False) for kb in range(4)] +
                                [lambda e: e.matmul(Ob[:, 0:16], Vs[0:16, s_, :], PTs[0:16, 4, :],
                                                    start=False, stop=True)] +
                                [lambda e, kb=kb: e.matmul(Ob[:, 128:144], ones_b[:, :], PTs[:, kb, :],
                                                           start=(kb == 0), stop=False, skip_group_check=True)
                                 for kb in range(4)] +
                                [lambda e: e.matmul(Ob[:, 128:144], ones_b[0:16, :], PTs[0:16, 4, :],
                                                    start=False, stop=True, skip_group_check=True)],
                                reads=[R_cvb, R_PTs, R_Vs, R_const], writes=[R_O])
                        T.op(DVE, lambda e: e.reciprocal(out=rsb[:, 0:16], in_=Ob[:, 128:144]),
                             reads=[R_O], writes=[R_rsb])
                        T.op(DVE, lambda e: e.tensor_tensor(out=atT[hp][:, 1024 + 16 * s_:1040 + 16 * s_],
                                                            in0=Ob[:, 0:16], in1=rsb[:, 0:16], op=ALU.mult),
                             reads=[R_O, R_rsb], writes=[R_atT[hp]], disjoint=True)

                    sm_S(0)
                    sm_S(1)
                    sm_PV(0)
                    sm_PV(1)
                    T.dma(SP, mix_scr[h], atT[hp][:, :], R_atT[hp], reads=[R_atT[hp]])
                T.barrier()
                if KSTOP == 2:
                    T.final()
                    return nc
                T.release(*R_kst, *R_vst, *R_atT, R_braw, R_mk, *R_ckfs, *R_cvbs)

            with ExitStack() as S3:
                pb = [ps(S3, f"pj{i}", [128, 512]) for i in range(4)]
                R_pb = [T.res(f"psum:pj{i}") for i in range(4)]
                tpf = [ps(S3, f"tpf{i}", [128, 512]) for i in range(2)]
                R_tpf = [T.res(f"psum:tpf{i}") for i in range(2)]
                uT = sb(S3, "uT", [128, NALL], F32)
                pa = sb(S3, "pa", [128, NALL], F32)
                pbuf = sb(S3, "pbuf", [128, NALL], F32)
                us = sb(S3, "us", [128, 2, 32], F32)
                qa = sb(S3, "qa", [128, 2, 32], F32)
                qb = sb(S3, "qb", [128, 2, 32], F32)
                dfix = sb(S3, "dfix", [128, 16], F32)
                dT = sb(S3, "dT", [128, 4, NM], BF16)
                poT = [sb(S3, f"poT{i}", [128, NM], BF16) for i in range(2)]
                spb = sb(S3, "spb", [15, 2, 2048], F32)
                ust = sb(S3, "ust", [48, 2048], F32)
                icn = sb(S3, "icn", [128, 64], F32)
                psc = sb(S3, "psc", [128, 16], F32)
                R_uT, R_pa, R_pbuf, R_us, R_qa, R_qb, R_dfix, R_dT = (
                    T.res(n) for n in ("uT", "pa", "pbuf", "us", "qa", "qb", "dfix", "dT"))
                R_poT = [T.res(f"poT{i}", dma=True) for i in range(2)]
                R_spb = T.res("spb", dma=True)
                R_ust = T.res("ust", dma=True)
                R_icn = T.res("icn", dma=True)
                T.dma(SP, spb[:, :, :], spd.rearrange("s r c -> r s c"), R_spb, writes=[R_spb])
                T.dma(SP, icn[:, :], invcnt[:, :], R_icn, writes=[R_icn])
                T.dma(SP, psc[:, :], pscale[:, :], R_icn, writes=[R_icn], disjoint=True)
                pcount = 0
                for g in range(4):
                    wwin = 2 ** (g + 1)
                    T.epoch(R_dT)
                    for i in range(2):
                        tU = ws.next()
                        for mt in range(2):
                            j = g * 4 + i * 2 + mt
                            T.epoch(R_uT)

                            def evac_u(ci, bank, R_bank, c0, n):
                                T.op(ACT, lambda e: e.copy(out=uT[:, c0:c0 + n], in_=bank[:, 0:n]),
                                     reads=[R_bank], writes=[R_uT], disjoint=True)
                            proj([(cg(tU), KT)], mt, xT, R_xT, CH_U, evac_u, pb, R_pb)
                            tb = state["alt"] % 2
                            state["alt"] += 1
                            T.group([lambda e, s=s, tb=tb: e.transpose(tpf[tb][:, s * 16:s * 16 + 15],
                                                                     spb[0:15, s, j * 128:(j + 1) * 128],
                                                                     ident_f[0:15, 0:15]) for s in range(2)] +
                                    [lambda e, tb=tb: e.transpose(tpf[tb][0:48, 128:256], uT[:, 1520:1568],
                                                                  ident_f[:, :])],
                                    reads=[R_spb, R_const, R_uT], writes=[R_tpf[tb]])
                            T.op(ACT, lambda e, tb=tb: e.copy(
                                out=us[:, :, 0:15], in_=tpf[tb][:, 0:32].rearrange("p (s n) -> p s n", n=16)[:, :, 0:15]),
                                reads=[R_tpf[tb]], writes=[R_us])
                            T.op(ACT, lambda e, tb=tb: e.copy(out=ust[0:48, j * 128:(j + 1) * 128],
                                                              in_=tpf[tb][0:48, 128:256]),
                                 reads=[R_tpf[tb]], writes=[R_ust], disjoint=True)
                            T.op(DVE, lambda e: e.tensor_copy(
                                out=us[:, :, 15:31], in_=uT[:, 1536:1568].rearrange("p (s n) -> p s n", n=16)),
                                reads=[R_uT], writes=[R_us], disjoint=True)
                            srcs = [(uT, R_uT), (pa, R_pa), (pbuf, R_pbuf), (pa, R_pa), (pbuf, R_pbuf)]
                            ssrc = [(us, R_us), (qa, R_qa), (qb, R_qb), (qa, R_qa), (qb, R_qb)]
                            lo, slo = 496, 0
                            for st in range(g + 1):
                                sh = 2 ** st
                                lo += sh
                                slo += sh
                                (a, Ra), (o, Ro) = srcs[st], srcs[st + 1]
                                T.op(DVE, lambda e, a=a, o=o, lo=lo, sh=sh: e.tensor_tensor(
                                    out=o[:, lo:1536], in0=a[:, lo:1536], in1=a[:, lo - sh:1536 - sh], op=ALU.add),
                                    reads=[Ra], writes=[Ro])
                                (a2, Ra2), (o2, Ro2) = ssrc[st], ssrc[st + 1]
                                T.op(DVE, lambda e, a2=a2, o2=o2, slo=slo, sh=sh: e.tensor_tensor(
                                    out=o2[:, :, slo:31], in0=a2[:, :, slo:31], in1=a2[:, :, slo - sh:31 - sh],
                                    op=ALU.add), reads=[Ra2], writes=[Ro2])
                            (wsu, Rws), (wss, Rwss) = srcs[g + 1], ssrc[g + 1]
                            jj = j % 4
                            T.op(DVE, lambda e: e.scalar_tensor_tensor(
                                out=dT[:, jj, 0:1024], in0=wsu[:, 512:1536], scalar=1.0 / wwin,
                                in1=uT[:, 512:1536], op0=ALU.mult, op1=ALU.subtract),
                                reads=[Rws, R_uT], writes=[R_dT], disjoint=True)
                            T.op(DVE, lambda e: e.tensor_tensor(out=dfix[:, :], in0=wsu[:, 512:528],
                                                                in1=icn[:, g * 16:(g + 1) * 16], op=ALU.mult),
                                 reads=[Rws, R_icn], writes=[R_dfix])
                            T.op(DVE, lambda e: e.tensor_tensor(out=dT[:, jj, 0:16], in0=dfix[:, :],
                                                                in1=uT[:, 512:528], op=ALU.subtract),
                                 reads=[R_dfix, R_uT, R_dT], writes=[R_dT], disjoint=True)
                            T.op(DVE, lambda e: e.scalar_tensor_tensor(
                                out=dT[:, jj, 1024:1056].rearrange("p (s n) -> p s n", n=16), in0=wss[:, :, 15:31],
                                scalar=1.0 / wwin, in1=us[:, :, 15:31], op0=ALU.mult, op1=ALU.subtract),
                                reads=[Rwss, R_us], writes=[R_dT], disjoint=True)
                    tP = ws.next()
                    for mt in range(4):
                        pp = pcount % 2
                        pcount += 1
                        T.epoch(R_poT[pp])

                        def evac_p(ci, bank, R_bank, c0, n):
                            T.op(DVE, lambda e: e.tensor_scalar(
                                out=poT[pp][:, c0:c0 + n], in0=bank[:, 0:n],
                                scalar1=psc[:, g * 4 + mt:g * 4 + mt + 1], scalar2=None, op0=ALU.mult),
                                reads=[R_bank, R_icn], writes=[R_poT[pp]], disjoint=True)
                        proj([(cg(tP), 4)], mt, dT, R_dT, CH_MAIN, evac_p, pb, R_pb)
                        T.dma(SP, mix_scr[16 + g * 4 + mt], poT[pp][:, :], R_poT[pp], reads=[R_poT[pp]])
                T.dma(SP, u_o[:, :], ust[:, :], R_ust, reads=[R_ust])
                T.barrier()
                if KSTOP == 3:
                    T.final()
                    return nc
                T.release(*R_poT, R_spb, R_ust, R_icn)

        def project_to_tokmajor(stack, n_slabs_m, get_segs, inT, R_in, ntok, chunks, scr, row0):
            pb = [ps(stack, f"pj{i}", [128, 512]) for i in range(4)]
            R_pb = [T.res(f"psum:pj{i}") for i in range(4)]
            tpf = [ps(stack, f"tpf{i}", [128, 512]) for i in range(3)]
            R_tpf = [T.res(f"psum:tpf{i}") for i in range(3)]
            zst = [sb(stack, f"zst{i}", [128, ntok], F32) for i in range(2)]
            R_zst = [T.res(f"zst{i}") for i in range(2)]
            ntile = (ntok + 127) // 128
            ost = [sb(stack, f"ost{i}", [128, ntile, 128], F32) for i in range(2)]
            R_ost = [T.res(f"ost{i}", dma=True) for i in range(2)]
            nfull = ntok // 128
            rem = ntok - nfull * 128
            scr_main = scr[row0:row0 + nfull * 128, :].rearrange("(t p) c -> p t c", p=128)
            tcount = 0
            for m in range(n_slabs_m):
                segs, mt = get_segs(m)
                zp = m % 2
                T.epoch(R_zst[zp])

                def evac_z(ci, bank, R_bank, c0, n):
                    T.op(ACT, lambda e: e.copy(out=zst[zp][:, c0:c0 + n], in_=bank[:, 0:n]),
                         reads=[R_bank], writes=[R_zst[zp]], disjoint=True)
                proj(segs, mt, inT, R_in, chunks, evac_z, pb, R_pb)
                T.epoch(R_ost[zp])
                for g4 in range((ntile + 3) // 4):
                    tiles = list(range(g4 * 4, min(g4 * 4 + 4, ntile)))
                    full = [r for r in tiles if r < nfull]
                    part = [r for r in tiles if r >= nfull]
                    tb = tcount % 3
                    tcount += 1
                    fns = []
                    for j, r in enumerate(tiles):
                        w = 128 if r < nfull else rem
                        fns.append(lambda e, j=j, r=r, w=w, tb=tb: e.transpose(
                            tpf[tb][0:w, j * 128:(j + 1) * 128], zst[zp][:, r * 128:r * 128 + w], ident_f[:, :]))
                    T.group(fns, reads=[R_zst[zp], R_const], writes=[R_tpf[tb]])
                    if full:
                        nf = len(full)
                        T.op(DVE, lambda e, tb=tb, nf=nf, f0=full[0]: e.tensor_copy(
                            out=ost[zp][:, f0:f0 + nf, :],
                            in_=tpf[tb][:, 0:nf * 128].rearrange("p (j n) -> p j n", n=128)),
                            reads=[R_tpf[tb]], writes=[R_ost[zp]], disjoint=True)
                    if part:
                        j = len(full)
                        T.op(DVE, lambda e, tb=tb, j=j, r=part[0]: e.tensor_copy(
                            out=ost[zp][0:rem, r, :], in_=tpf[tb][0:rem, j * 128:(j + 1) * 128]),
                            reads=[R_tpf[tb]], writes=[R_ost[zp]], disjoint=True)
                T.dma(SP, scr_main[:, :, m * 128:(m + 1) * 128], ost[zp][:, 0:nfull, :], R_ost[zp], reads=[R_ost[zp]])
                if rem:
                    T.dma(SP, scr[row0 + nfull * 128:row0 + ntok, m * 128:(m + 1) * 128], ost[zp][0:rem, nfull, :],
                          R_ost[zp], reads=[R_ost[zp]])
            return R_ost

        with ExitStack() as S4:
            mixT = sb(S4, "mixT", [128, KT, NM], BF16)
            R_mix = T.res("mixT", dma=True)
            for q4 in range(4):
                T.dma(SP, mixT[:, q4 * 8:(q4 + 1) * 8, :], mix_scr[q4 * 8:(q4 + 1) * 8].rearrange("j p n -> p j n"),
                      R_mix, writes=[R_mix], disjoint=(q4 > 0))
            cur = {}

            def segs_out(m):
                if m % 2 == 0:
                    cur["s"] = ws.next()
                return [(cg(cur["s"]), KT)], m % 2
            R_o = project_to_tokmajor(S4, 32, segs_out, mixT, R_mix, NM, CH_MAIN, z1_scr, 0)
            T.barrier()
            if KSTOP == 4:
                T.final()
                return nc
            T.release(R_mix, *R_o)

        def layer_norm_pass(stack, zsrc, xsrc, xrow0, g_in, b_in, out_dram, after_tile):
            zts = [sb(stack, f"zt{i}", [128, D], F32) for i in range(2)]
            xt = sb(stack, "xt", [128, D], F32)
            gt = sb(stack, "gt", [128, D], F32)
            bt = sb(stack, "bt", [128, D], F32)
            sts = [sb(stack, f"st{i}", [128, 64], F32) for i in range(2)]
            R_zts = [T.res(f"zt{i}", dma=True) for i in range(2)]
            R_xt, R_gb = T.res("xt", dma=True), T.res("gb", dma=True)
            R_sts = [T.res(f"st{i}") for i in range(2)]
            T.dma(SP, gt[:, :], g_in[:, :], R_gb, writes=[R_gb])
            T.dma(SP, bt[:, :], b_in[:, :], R_gb, writes=[R_gb], disjoint=True)
            def geom(r):
                return (128 if r < 8 else 32), r * 128

            def load_z(r):
                rows, r0 = geom(r)
                T.dma(SP, zts[r % 2][0:rows, :], zsrc[r0:r0 + rows, :], R_zts[r % 2], writes=[R_zts[r % 2]])

            def load_x(r):
                rows, r0 = geom(r)
                T.dma(SP, xt[0:rows, :], xsrc[xrow0 + r0:xrow0 + r0 + rows, :], R_xt, writes=[R_xt])

            load_z(0)
            load_x(0)
            for r in range(9):
                rows, r0 = geom(r)
                zt, R_zt, st, R_st = zts[r % 2], R_zts[r % 2], sts[r % 2], R_sts[r % 2]
                if r + 1 < 9:
                    load_z(r + 1)
                T.op(DVE, lambda e: e.scalar_tensor_tensor(out=zt[0:rows, :], in0=xt[0:rows, :], scalar=ALPHA,
                                                           in1=zt[0:rows, :], op0=ALU.mult, op1=ALU.add),
                     reads=[R_xt, R_zt], writes=[R_zt])
                if r + 1 < 9:
                    load_x(r + 1)
                for c in range(8):
                    T.op(DVE, lambda e, c=c: e.bn_stats(st[0:rows, c * 6:(c + 1) * 6], zt[0:rows, c * 512:(c + 1) * 512]),
                         reads=[R_zt], writes=[R_st], disjoint=(c > 0))
                T.op(DVE, lambda e: e.bn_aggr(st[0:rows, 48:50], st[0:rows, 0:48]), reads=[R_st], writes=[R_st])
                T.op(DVE, lambda e: e.tensor_scalar(out=st[0:rows, 50:51], in0=st[0:rows, 49:50], scalar1=EPS,
                                                    scalar2=None, op0=ALU.add), reads=[R_st], writes=[R_st])
                T.op(ACT, lambda e: e.activation(out=st[0:rows, 51:52], in_=st[0:rows, 50:51], func=AF.Sqrt),
                     reads=[R_st], writes=[R_st])
                T.op(DVE, lambda e: e.reciprocal(out=st[0:rows, 52:53], in_=st[0:rows, 51:52]),
                     reads=[R_st], writes=[R_st])
                T.op(DVE, lambda e: e.scalar_tensor_tensor(out=st[0:rows, 53:54], in0=st[0:rows, 48:49], scalar=-1.0,
                                                           in1=st[0:rows, 52:53], op0=ALU.mult, op1=ALU.mult),
                     reads=[R_st], writes=[R_st])
                T.op(ACT, lambda e: e.activation(out=zt[0:rows, :], in_=zt[0:rows, :], func=AF.Identity,
                                                 bias=st[0:rows, 53:54], scale=st[0:rows, 52:53]),
                     reads=[R_st, R_zt], writes=[R_zt])
                T.op(DVE, lambda e: e.tensor_tensor(out=zt[0:rows, :], in0=zt[0:rows, :], in1=gt[0:rows, :],
                                                    op=ALU.mult), reads=[R_zt, R_gb], writes=[R_zt])
                T.op(DVE, lambda e: e.tensor_tensor(out=zt[0:rows, :], in0=zt[0:rows, :], in1=bt[0:rows, :],
                                                    op=ALU.add), reads=[R_zt, R_gb], writes=[R_zt])
                T.dma(SP, out_dram[r0:r0 + rows, :], zt[0:rows, :], R_zt, reads=[R_zt])
                if after_tile is not None:
                    after_tile(r, rows, r0, zt, R_zt)
            return [*R_zts, R_xt, R_gb]

        with ExitStack() as SB_:
            x1T = sb(SB_, "x1T", [128, KT, NM], BF16)
            R_x1T = T.res("x1T")
            with ExitStack() as S5:
                tpf = [ps(S5, f"tpf{i}", [128, 512]) for i in range(6)]
                R_tpf = [T.res(f"psum:tpf{i}") for i in range(6)]
                cc = {"n": 0}

                def after1(r, rows, r0, zt, R_zt):
                    for g8 in range(8):
                        b = cc["n"] % 6
                        cc["n"] += 1
                        T.group([lambda e, j=j, b=b: e.transpose(
                            tpf[b][:, j * 128:j * 128 + rows], zt[0:rows, (g8 * 4 + j) * 128:(g8 * 4 + j + 1) * 128],
                            ident_f[0:rows, 0:rows]) for j in range(4)],
                            reads=[R_zt, R_const], writes=[R_tpf[b]])
                        T.op(ACT, lambda e, b=b: e.copy(
                            out=x1T[:, g8 * 4:(g8 + 1) * 4, r0:r0 + rows],
                            in_=tpf[b][:, :].rearrange("p (j n) -> p j n", n=128)[:, :, 0:rows]),
                            reads=[R_tpf[b]], writes=[R_x1T], disjoint=True)
                rel = layer_norm_pass(S5, z1_scr, xc, NH, g1, b1, x1_scr, after1)
                T.barrier()
                if KSTOP == 5:
                    T.final()
                    return nc
                T.release(*rel)

            with ExitStack() as S6:
                pb = [ps(S6, f"pj{i}", [128, 512]) for i in range(6)]
                R_pb = [T.res(f"psum:pj{i}") for i in range(6)]
                sg = [sb(S6, f"sg{i}", [128, 352], F32) for i in range(2)]
                R_sg = [T.res(f"sg{i}") for i in range(2)]
                hst = [sb(S6, f"hst{i}", [128, NM], BF16) for i in range(2)]
                R_hst = [T.res(f"hst{i}", dma=True) for i in range(2)]
                n_sg = 0
                for j in range(KTF):
                    slabGU, R_GU = ws.next()
                    hp = j % 2
                    T.epoch(R_hst[hp])
                    for ci, (c0, n) in enumerate(CH_MAIN):
                        bg = state["bank"] % 6
                        bu = (state["bank"] + 1) % 6
                        state["bank"] += 2
                        T.group([lambda e, k=k: e.matmul(pb[bg][:, 0:n], slabGU[:, k, 0:128],
                                                         x1T[:, k, c0:c0 + n], start=(k == 0),
                                                         stop=(k == KT - 1)) for k in range(KT)],
                                reads=[R_GU, R_x1T], writes=[R_pb[bg]])
                        T.group([lambda e, k=k: e.matmul(pb[bu][:, 0:n], slabGU[:, k, 128:256],
                                                         x1T[:, k, c0:c0 + n], start=(k == 0),
                                                         stop=(k == KT - 1)) for k in range(KT)],
                                reads=[R_GU, R_x1T], writes=[R_pb[bu]])
                        sp_ = n_sg % 2
                        n_sg += 1
                        T.op(ACT, lambda e: e.activation(out=sg[sp_][:, 0:n], in_=pb[bg][:, 0:n], func=AF.Silu),
                             reads=[R_pb[bg]], writes=[R_sg[sp_]])
                        T.op(DVE, lambda e: e.tensor_tensor(
                            out=hst[hp][:, c0:c0 + n], in0=pb[bu][:, 0:n], in1=sg[sp_][:, 0:n], op=ALU.mult),
                            reads=[R_pb[bu], R_sg[sp_]], writes=[R_hst[hp]], disjoint=True)
                    T.dma(SP, hT_scr[j], hst[hp][:, :], R_hst[hp], reads=[R_hst[hp]])
                T.barrier()
                if KSTOP == 6:
                    T.final()
                    return nc
                T.release(*R_hst)

        for hf in range(2):
            with ExitStack() as S7:
                hT = sb(S7, "hT", [128, KTF, 528], BF16)
                R_hTs = [T.res("hTlo", dma=True), T.res("hThi", dma=True)]
                for part, (ka, kb_) in enumerate(((0, 43), (43, KTF))):
                    for q4 in range(ka, kb_, 8):
                        q5 = min(q4 + 8, kb_)
                        T.dma(SP, hT[:, q4:q5, :],
                              hT_scr[q4:q5, :, hf * 528:(hf + 1) * 528].rearrange("j p n -> p j n"),
                              R_hTs[part], writes=[R_hTs[part]], disjoint=(q4 > ka))

                def segs_down(m):
                    return [(ws.next, 43), (ws.next, 43)], 0
                R_o = project_to_tokmajor(S7, 32, segs_down, hT, R_hTs, 528, [(0, 264), (264, 264)], z2_scr, hf * 528)
                T.barrier()
                if KSTOP == 7:
                    T.final()
                    return nc
                T.release(*R_hTs, *R_o)

        with ExitStack() as S8:
            rel = layer_norm_pass(S8, z2_scr, x1_scr, 0, g2, b2, y_o, None)
            T.barrier()
            if KSTOP == 8:
                T.final()
                return nc
            T.release(*rel)
        T.final()
    return nc


_NC_CACHE = {}


def _host_tables():
    j = np.arange(128)[:, None, None]
    kb = np.arange(5)[None, :, None]
    i = np.arange(128)[None, None, :]
    dist = 512 - 128 * kb + i - j
    idx = np.clip(dist, -256, 256) + 256
    kk = kb * 128 + j
    vis = np.where(i < 64, kk < 576, kk >= 64)
    maskT = np.where(vis, 0.0, NEG).astype(np.float32).reshape(128, 640)
    return idx.reshape(128, 640), maskT


def kernel(x_prompt, x_sample, cache_k, cache_v, state_pool, w_in, rel_bias, w_pool, pool_scale,
           w_out, ln1_g, ln1_b, w_gate, w_up, w_down, ln2_g, ln2_b):
    f = lambda a: np.ascontiguousarray(np.asarray(a, dtype=np.float32))
    x_prompt, x_sample, cache_k, cache_v, state_pool = map(f, (x_prompt, x_sample, cache_k, cache_v, state_pool))
    if "nc" not in _NC_CACHE:
        _NC_CACHE["nc"] = build_nc()
    nc = _NC_CACHE["nc"]
    idx, maskT = _host_tables()
    biasT = f(np.asarray(rel_bias, np.float32)[0][:, idx])
    common = {
        "w_in": f(w_in[0]), "w_out": f(w_out[0]), "w_gate": f(w_gate[0]), "w_up": f(w_up[0]), "w_down": f(w_down[0]),
        "w_pool": f(w_pool[0]),
        "pscale": f(np.asarray(pool_scale, np.float32)[0].reshape(16, 128).T),
        "g1": f(np.broadcast_to(np.asarray(ln1_g, np.float32)[0][None, :], (128, D))),
        "b1": f(np.broadcast_to(np.asarray(ln1_b, np.float32)[0][None, :], (128, D))),
        "g2": f(np.broadcast_to(np.asarray(ln2_g, np.float32)[0][None, :], (128, D))),
        "b2": f(np.broadcast_to(np.asarray(ln2_b, np.float32)[0][None, :], (128, D))),
        "biasT": biasT, "maskT": maskT, "ident": np.eye(128, dtype=np.float32),
    }
    wins = np.array([2, 4, 8, 16])
    in_maps = []
    for c in range(NCORES):
        b, half = c // 2, c % 2
        xc = np.zeros((NALL, D), np.float32)
        if half == 1:
            xc[0:NH] = x_prompt[b, 512:1024]
        xc[NH:NH + NP] = x_prompt[b, half * 1024:(half + 1) * 1024]
        xc[NH + NP:NH + NP + 16] = x_sample[2 * c]
        xc[NH + NP + 16:NALL] = x_sample[2 * c + 1]
        t = np.arange(16)
        if half == 0:
            cntv = np.minimum(wins[:, None], t[None, :] + 1).astype(np.float32)
        else:
            cntv = np.broadcast_to(wins[:, None].astype(np.float32), (4, 16))
        inv = (1.0 / cntv).astype(np.float32).reshape(1, 64)
        m = dict(common)
        m.update({
            "xc": xc,
            "ck": f(cache_k[0, 2 * c:2 * c + 2].reshape(2, 512, 2048)),
            "cv": f(cache_v[0, 2 * c:2 * c + 2].reshape(2, 512, 2048)),
            "sp": f(state_pool[0, 2 * c:2 * c + 2]),
            "halo": np.full((128, 1), 0.0 if half == 1 else NEG, np.float32),
            "invcnt": f(np.broadcast_to(inv, (128, 64))),
        })
        in_maps.append(m)
    if _NC_CACHE.get("prep_only"):
        return in_maps
    res = run_bass_kernel_spmd(nc, in_maps, core_ids=list(range(NCORES)))
    R = res.results
    y_prompt = np.zeros((4, 2048, D), np.float32)
    y_sample = np.zeros((16, 16, D), np.float32)
    kp = np.zeros((1, 4, 512, 16, 128), np.float32)
    vp = np.zeros((1, 4, 512, 16, 128), np.float32)
    pp = np.zeros((1, 4, 15, 2048), np.float32)
    ksn = np.zeros((1, 16, 16, 16, 128), np.float32)
    vsn = np.zeros((1, 16, 16, 16, 128), np.float32)
    psn = np.zeros((1, 16, 15, 2048), np.float32)
    for c in range(NCORES):
        b, half = c // 2, c % 2
        y = np.asarray(R[c]["y"])
        ko = np.asarray(R[c]["k_out"])
        vo = np.asarray(R[c]["v_out"])
        uo = np.asarray(R[c]["u_out"])
        y_prompt[b, half * 1024:(half + 1) * 1024] = y[0:1024]
        for s in range(2):
            y_sample[2 * c + s] = y[1024 + 16 * s:1040 + 16 * s]
            ksn[0, 2 * c + s] = ko[512 + 16 * s:528 + 16 * s].reshape(16, 16, 128)
            vsn[0, 2 * c + s] = vo[512 + 16 * s:528 + 16 * s].reshape(16, 16, 128)
            psn[0, 2 * c + s] = uo[16 + 16 * s + 1:16 + 16 * s + 16]
        if half == 1:
            kp[0, b] = ko[0:512].reshape(512, 16, 128)
            vp[0, b] = vo[0:512].reshape(512, 16, 128)
            pp[0, b] = uo[1:16]
    return (y_prompt, y_sample, kp, vp, pp, ksn, vsn, psn)
```

```python
from contextlib import ExitStack

import numpy as np
import concourse.bass as bass
import concourse.mybir as mybir
from concourse.bass_utils import run_bass_kernel_spmd

F32 = mybir.dt.float32
BF16 = mybir.dt.bfloat16
ALU = mybir.AluOpType
AF = mybir.ActivationFunctionType
AX = mybir.AxisListType

NCORES = 8
D = 4096
DFF = 11008
KT = D // 128
KTF = DFF // 128
NH = 512
NP = 1024
NS = 32
NM = NP + NS
NALL = NH + NM
NKO = 544
NEG = -30000.0
ALPHA = 2.0 ** 0.25
SCALE = 128.0 ** -0.5
EPS = 1e-5
SLOT = 8192
NSLOT = 3
import os
KSTOP = int(os.environ.get("KSTOP", "0"))


class Q:
    def __init__(self, name, eng, sem, is_pe=False):
        self.name, self.eng, self.sem, self.count, self.waited, self.is_pe = name, eng, sem, 0, {}, is_pe


class DSem:
    def __init__(self, sem):
        self.sem, self.count = sem, 0


class Res:
    def __init__(self, name, dsem=None):
        self.name, self.w, self.r, self.d = name, {}, {}, dsem
        self.excl = name.startswith("psum:")


def _merge(dst, tok):
    k = id(tok[0])
    if k not in dst or dst[k][1] < tok[1]:
        dst[k] = tok


class Tracker:
    def __init__(self, nc, stack):
        self.nc = nc
        mk = lambda n: stack.enter_context(nc.semaphore(n))
        self.PE = Q("pe", nc.tensor, mk("s_pe"), True)
        self.ACT = Q("act", nc.scalar, mk("s_act"))
        self.DVE = Q("dve", nc.vector, mk("s_dve"))
        self.POOL = Q("pool", nc.gpsimd, mk("s_pool"))
        self.SP = Q("sp", nc.sync, mk("s_sp"))
        self.queues = [self.PE, self.ACT, self.DVE, self.POOL, self.SP]
        self.free_hw = [DSem(mk(f"s_h{i}")) for i in range(30)]
        self.free_sw = [DSem(mk(f"s_w{i}")) for i in range(10)]
        self.all_dsems = self.free_hw + self.free_sw

    def res(self, name, dma=False):
        if not dma:
            return Res(name, None)
        r = Res(name, (self.free_sw if dma == "sw" else self.free_hw).pop())
        r.kind = "sw" if dma == "sw" else "hw"
        return r

    def release(self, *rs):
        for r in rs:
            if r.d is not None:
                (self.free_sw if r.kind == "sw" else self.free_hw).append(r.d)
                r.d = None

    def _wait(self, q, toks):
        for sem, v in toks:
            k = id(sem)
            if q.waited.get(k, 0) >= v:
                continue
            if sem is q.sem and q.is_pe:
                continue
            q.eng.wait_ge(sem, v)
            q.waited[k] = v

    def _deps(self, reads, writes, disjoint):
        deps = []
        for r in reads:
            deps += list(r.w.values())
            if r.excl:
                deps += list(r.r.values())
        for w in writes:
            deps += list(w.r.values())
            if not disjoint:
                deps += list(w.w.values())
        return deps

    def _commit(self, tok, reads, writes, disjoint):
        for r in reads:
            _merge(r.r, tok)
        for w in writes:
            if disjoint:
                _merge(w.w, tok)
            else:
                for t in w.w.values():
                    _merge(w.r, t)
                w.w = {id(tok[0]): tok}

    def epoch(self, *rs):
        for r in rs:
            for t in r.w.values():
                _merge(r.r, t)
            r.w = {}

    def op(self, q, fn, reads=(), writes=(), disjoint=False):
        self._wait(q, self._deps(reads, writes, disjoint))
        ins = fn(q.eng)
        q.count += 1
        ins.then_inc(q.sem, 1)
        self._commit((q.sem, q.count), reads, writes, disjoint)

    def group(self, fns, reads=(), writes=(), disjoint=False):
        q = self.PE
        self._wait(q, self._deps(reads, writes, disjoint))
        ins = None
        for fn in fns:
            ins = fn(q.eng)
        q.count += 1
        ins.then_inc(q.sem, 1)
        self._commit((q.sem, q.count), reads, writes, disjoint)

    def dma(self, q, out, in_, owner, reads=(), writes=(), disjoint=False):
        assert (owner.kind == "sw") == (q is self.POOL), owner.name
        self._wait(q, self._deps(reads, writes, disjoint))
        ins = q.eng.dma_start(out=out, in_=in_)
        owner.d.count += 16
        ins.then_inc(owner.d.sem, 16)
        self._commit((owner.d.sem, owner.d.count), reads, writes, disjoint)

    def barrier(self):
        toks = [(q.sem, q.count) for q in self.queues if q.count > 0]
        toks += [(d.sem, d.count) for d in self.all_dsems if d.count > 0]
        for q in self.queues:
            self._wait(q, toks)

    def final(self):
        toks = [(q.sem, q.count) for q in self.queues if q.count > 0]
        toks += [(d.sem, d.count) for d in self.all_dsems if d.count > 0]
        self._wait(self.SP, toks)


def build_nc():
    nc = bass.Bass("TRN2", target_bir_lowering=False)

    def din(name, shape, dt=F32):
        return nc.dram_tensor(name, list(shape), dt, kind="ExternalInput").ap()

    def dout(name, shape, dt=F32):
        return nc.dram_tensor(name, list(shape), dt, kind="ExternalOutput").ap()

    def dscr(name, shape, dt=F32):
        return nc.dram_tensor(name, list(shape), dt, kind="Internal").ap()

    xc = din("xc", [NALL, D])
    ck = din("ck", [2, 512, 2048])
    cv = din("cv", [2, 512, 2048])
    spd = din("sp", [2, 15, 2048])
    w_in = din("w_in", [D, 8192])
    w_out = din("w_out", [D, D])
    w_gate = din("w_gate", [D, DFF])
    w_up = din("w_up", [D, DFF])
    w_down = din("w_down", [DFF, D])
    w_pool = din("w_pool", [4, 512, 512])
    pscale = din("pscale", [128, 16])
    g1 = din("g1", [128, D])
    b1 = din("b1", [128, D])
    g2 = din("g2", [128, D])
    b2 = din("b2", [128, D])
    biasT = din("biasT", [16, 128, 640])
    maskT = din("maskT", [128, 640])
    halo = din("halo", [128, 1])
    invcnt = din("invcnt", [128, 64])

    y_o = dout("y", [NM, D])
    k_o = dout("k_out", [NKO, 2048])
    v_o = dout("v_out", [NKO, 2048])
    u_o = dout("u_out", [48, 2048])

    mix_scr = dscr("mix_scr", [32, 128, NM], BF16)
    z1_scr = dscr("z1_scr", [NM, D])
    x1_scr = dscr("x1_scr", [NM, D])
    hT_scr = dscr("hT_scr", [KTF, 128, NM], BF16)
    z2_scr = dscr("z2_scr", [NM, D])

    w_in_v = w_in.rearrange("(k p) n -> p k n", p=128)
    w_out_v = w_out.rearrange("(k p) n -> p k n", p=128)
    w_gate_v = w_gate.rearrange("(k p) n -> p k n", p=128)
    w_up_v = w_up.rearrange("(k p) n -> p k n", p=128)
    w_down_v = w_down.rearrange("(k p) n -> p k n", p=128)

    slabs = []
    for h in range(16):
        slabs.append((KT, 256, [(0, w_in_v[:, :, h * 128:(h + 1) * 128]),
                                (128, w_in_v[:, :, 2048 + h * 128:2048 + (h + 1) * 128])]))
        slabs.append((KT, 128, [(0, w_in_v[:, :, 4096 + h * 128:4096 + (h + 1) * 128])]))
    for g in range(4):
        for i in range(2):
            c0 = 6144 + (g * 2 + i) * 256
            slabs.append((KT, 256, [(0, w_in_v[:, :, c0:c0 + 256])]))
        slabs.append((4, 512, [(0, w_pool[g].rearrange("(k p) n -> p k n", p=128))]))
    for i in range(16):
        slabs.append((KT, 256, [(0, w_out_v[:, :, i * 256:(i + 1) * 256])]))
    for j in range(KTF):
        slabs.append((KT, 256, [(0, w_gate_v[:, :, j * 128:(j + 1) * 128]),
                                (128, w_up_v[:, :, j * 128:(j + 1) * 128])]))
    for hf in range(2):
        for m in range(32):
            slabs.append((43, 128, [(0, w_down_v[:, 0:43, m * 128:(m + 1) * 128])]))
            slabs.append((43, 128, [(0, w_down_v[:, 43:86, m * 128:(m + 1) * 128])]))

    with ExitStack() as G:
        T = Tracker(nc, G)
        PE, ACT, DVE, POOL, SP = T.PE, T.ACT, T.DVE, T.POOL, T.SP

        uid = {"n": 0}

        def sb(stack, name, shape, dt):
            uid["n"] += 1
            return stack.enter_context(nc.sbuf_tensor(f"{name}_{uid['n']}", list(shape), dt))

        def ps(stack, name, shape, dt=F32):
            uid["n"] += 1
            return stack.enter_context(nc.psum_tensor(f"{name}_{uid['n']}", list(shape), dt))

        ring = [sb(G, f"ring{i}", [128, SLOT], BF16) for i in range(NSLOT)]
        R_ring = [T.res(f"ring{i}", dma="sw") for i in range(NSLOT)]
        ident_f = sb(G, "ident_f", [128, 128], F32)
        ident_b = sb(G, "ident_b", [128, 128], BF16)
        ones_b = sb(G, "ones_b", [128, 128], BF16)
        R_const = T.res("const", dma=True)

        ident_in = din("ident", [128, 128])
        T.dma(SP, ident_f[:, :], ident_in[:, :], R_const, writes=[R_const])
        T.op(DVE, lambda e: e.tensor_copy(out=ident_b[:, :], in_=ident_f[:, :]), reads=[R_const], writes=[R_const])
        T.op(DVE, lambda e: e.memset(ones_b[:, :], 1.0), writes=[R_const], disjoint=True)

        class WS:
            def __init__(self):
                self.issued = 0
                self.next_i = 0

            def _issue(self, i):
                kt, ncols, pieces = slabs[i]
                s = i % NSLOT
                view = ring[s][:, 0:kt * ncols].rearrange("p (k n) -> p k n", n=ncols)
                first = True
                for off, src in pieces:
                    pc = src.shape[-1]
                    T.dma(POOL, view[:, :, off:off + pc], src, R_ring[s], writes=[R_ring[s]], disjoint=not first)
                    first = False

            def prefetch(self, upto):
                while self.issued < min(upto, len(slabs)):
                    self._issue(self.issued)
                    self.issued += 1

            def next(self):
                i = self.next_i
                self.next_i += 1
                self.prefetch(i + NSLOT)
                kt, ncols, _ = slabs[i]
                s = i % NSLOT
                return ring[s][:, 0:kt * ncols].rearrange("p (k n) -> p k n", n=ncols), R_ring[s]

        ws = WS()

        state = {"bank": 0, "alt": 0}

        def proj(segs, mt, inT, R_in, chunks, evac, banks, R_banks, in_off=0):
            nb = len(banks)
            bs = []
            for _ in chunks:
                bs.append(state["bank"] % nb)
                state["bank"] += 1
            ktot = sum(kt for _, kt in segs)
            kk0 = 0
            for si, (getter, kt) in enumerate(segs):
                slab, R_slab = getter()
                for ci, (c0, n) in enumerate(chunks):
                    b = bs[ci]
                    fns = []
                    for k in range(kt):
                        kk = kk0 + k
                        fns.append(lambda e, k=k, kk=kk: e.matmul(
                            banks[b][:, 0:n], slab[:, k, mt * 128:(mt + 1) * 128],
                            inT[:, kk, in_off + c0:in_off + c0 + n], start=(kk == 0), stop=(kk == ktot - 1)))
                    T.group(fns, reads=[R_slab, R_in[si] if isinstance(R_in, list) else R_in],
                            writes=[R_banks[b]], disjoint=(si > 0))
                    if si == len(segs) - 1:
                        evac(ci, banks[b], R_banks[b], c0, n)
                kk0 += kt

        def cg(t):
            return lambda: t

        CH_ALL = [(392 * i, 392) for i in range(4)]
        CH_MAIN = [(352 * i, 352) for i in range(3)]
        CH_U = [(496 + 268 * i, 268) for i in range(4)]

        with ExitStack() as SA_:
            xT = sb(SA_, "xT", [128, KT, NALL], BF16)
            R_xT = T.res("xT")
            with ExitStack() as S1:
                xb = [sb(S1, f"xb{i}", [128, D], BF16) for i in range(4)]
                R_xb = [T.res(f"xb{i}", dma="sw") for i in range(4)]
                tpb = [ps(S1, f"tpb{i}", [128, 1024], BF16) for i in range(2)]
                R_tpb = [T.res(f"psum:tpb{i}") for i in range(2)]
                ws.prefetch(NSLOT)
                cnt = 0
                for r in range(13):
                    rows = 128 if r < 12 else 32
                    r0 = r * 128
                    T.dma(POOL, xb[r % 4][0:rows, :], xc[r0:r0 + rows, :], R_xb[r % 4], writes=[R_xb[r % 4]])
                    for g4 in range(4):
                        b = cnt % 2
                        cnt += 1
                        fns = []
                        for j in range(8):
                            k = g4 * 8 + j
                            fns.append(lambda e, j=j, k=k, b=b: e.transpose(
                                tpb[b][:, j * 128:j * 128 + rows], xb[r % 4][0:rows, k * 128:(k + 1) * 128],
                                ident_b[0:rows, 0:rows]))
                        T.group(fns, reads=[R_xb[r % 4], R_const], writes=[R_tpb[b]])
                        src = tpb[b][:, :].rearrange("p (j n) -> p j n", n=128)[:, :, 0:rows]
                        dst = xT[:, g4 * 8:(g4 + 1) * 8, r0:r0 + rows]
                        if cnt % 2 == 0:
                            T.op(DVE, lambda e, s=src, d=dst: e.tensor_copy(out=d, in_=s),
                                 reads=[R_tpb[b]], writes=[R_xT], disjoint=True)
                        else:
                            T.op(ACT, lambda e, s=src, d=dst: e.copy(out=d, in_=s),
                                 reads=[R_tpb[b]], writes=[R_xT], disjoint=True)
                T.barrier()
                if KSTOP == 1:
                    T.final()
                    return nc
                T.release(*R_xb)

            with ExitStack() as S2:
                pb = [ps(S2, f"pj{i}", [128, 512]) for i in range(2)]
                R_pb = [T.res(f"psum:pj{i}") for i in range(2)]
                tpf = [ps(S2, f"tpf{i}", [128, 512]) for i in range(2)]
                R_tpf = [T.res(f"psum:tpf{i}") for i in range(2)]
                SAb = ps(S2, "SA", [128, 512])
                SBb = ps(S2, "SB", [128, 512])
                Obs = [ps(S2, f"O{i}", [128, 512]) for i in range(2)]
                R_Os = [T.res(f"psum:O{i}") for i in range(2)]
                R_SA, R_SB = T.res("psum:SA"), T.res("psum:SB")
                QT = sb(S2, "QT", [128, NM], BF16)
                KTb = sb(S2, "KT", [128, NALL], BF16)
                KTf = sb(S2, "KTf", [128, NKO], F32)
                VTf = sb(S2, "VTf", [128, NALL], F32)
                Vb = sb(S2, "Vb", [128, 13, 128], BF16)
                Vs = sb(S2, "Vs", [16, 2, 128], BF16)
                kst = [sb(S2, f"kst{i}", [128, 5, 128], F32) for i in range(2)]
                vst = [sb(S2, f"vst{i}", [128, 5, 128], F32) for i in range(2)]
                atT = [sb(S2, f"atT{i}", [128, NM], BF16) for i in range(2)]
                braw = sb(S2, "braw", [128, 640], F32)
                bm = sb(S2, "bm", [128, 640], F32)
                bh = sb(S2, "bh", [128, 640], F32)
                mk = sb(S2, "mk", [128, 640], F32)
                hl = sb(S2, "hl", [128, 1], F32)
                tmp = [sb(S2, f"tmp{i}", [128, 640], F32) for i in range(2)]
                PT = [sb(S2, f"PT{i}", [128, 640], BF16) for i in range(2)]
                rsbs = [sb(S2, f"rsb{i}", [128, 128], F32) for i in range(2)]
                ckfs = [sb(S2, f"ckf{i}", [128, 4, 128], F32) for i in range(2)]
                ckTs = [sb(S2, f"ckT{i}", [128, 512], BF16) for i in range(2)]
                cvbs = [sb(S2, f"cvb{i}", [128, 4, 128], BF16) for i in range(2)]
                tmpss = [sb(S2, f"tmps{i}", [128, 5, 16], F32) for i in range(2)]
                PTss = [sb(S2, f"PTs{i}", [128, 5, 16], BF16) for i in range(2)]
                R_QT, R_KT, R_KTf, R_VTf, R_Vb, R_Vs = (T.res(n) for n in ("QT", "KT", "KTf", "VTf", "Vb", "Vs"))
                R_kst = [T.res(f"kst{i}", dma=True) for i in range(2)]
                R_vst = [T.res(f"vst{i}", dma=True) for i in range(2)]
                R_atT = [T.res(f"atT{i}", dma=True) for i in range(2)]
                R_braw = T.res("braw", dma=True)
                R_bm, R_bh = T.res("bm"), T.res("bh")
                R_mk = T.res("mk", dma=True)
                R_tmp = [T.res(f"tmp{i}") for i in range(2)]
                R_PT = [T.res(f"PT{i}") for i in range(2)]
                R_rsbs = [T.res(f"rsb{i}") for i in range(2)]
                R_ckfs = [T.res(f"ckf{i}", dma=True) for i in range(2)]
                R_ckTs = [T.res(f"ckT{i}") for i in range(2)]
                R_cvbs = [T.res(f"cvb{i}", dma="sw") for i in range(2)]
                R_tmpss = [T.res(f"tmps{i}") for i in range(2)]
                R_PTss = [T.res(f"PTs{i}") for i in range(2)]

                T.dma(SP, mk[:, :], maskT[:, :], R_mk, writes=[R_mk])
                T.dma(SP, hl[:, :], halo[:, :], R_mk, writes=[R_mk], disjoint=True)

                k_o_main = k_o[0:512, :].rearrange("(t p) c -> p t c", p=128)
                v_o_main = v_o[0:512, :].rearrange("(t p) c -> p t c", p=128)

                for h in range(16):
                    hp = h % 2
                    tA = ws.next()
                    T.epoch(R_QT, R_KT, R_KTf, R_VTf, R_Vb, R_Vs)
                    for s_ in range(2):
                        T.dma(SP, ckfs[s_][:, :, :],
                              ck[s_, :, h * 128:(h + 1) * 128].rearrange("(t p) d -> p t d", p=128),
                              R_ckfs[s_], writes=[R_ckfs[s_]])
                        T.dma(POOL, cvbs[s_][:, :, :],
                              cv[s_, :, h * 128:(h + 1) * 128].rearrange("(t p) d -> p t d", p=128),
                              R_cvbs[s_], writes=[R_cvbs[s_]])
                    T.dma(SP, braw[:, :], biasT[h], R_braw, writes=[R_braw])
                    T.op(DVE, lambda e: e.tensor_tensor(out=bm[:, :], in0=braw[:, :], in1=mk[:, :], op=ALU.add),
                         reads=[R_braw, R_mk], writes=[R_bm])
                    T.op(DVE, lambda e: e.tensor_scalar(out=bh[:, 0:512], in0=bm[:, 0:512], scalar1=hl[:, 0:1],
                                                        scalar2=None, op0=ALU.add),
                         reads=[R_bm, R_mk], writes=[R_bh])

                    def evac_q(ci, bank, R_bank, c0, n):
                        T.op(ACT, lambda e: e.copy(out=QT[:, c0:c0 + n], in_=bank[:, 0:n]),
                             reads=[R_bank], writes=[R_QT], disjoint=True)
                    proj([(cg(tA), KT)], 0, xT, R_xT, CH_MAIN, evac_q, pb, R_pb, in_off=NH)

                    def evac_k(ci, bank, R_bank, c0, n):
                        T.op(DVE, lambda e: e.tensor_copy(out=KTb[:, c0:c0 + n], in_=bank[:, 0:n]),
                             reads=[R_bank], writes=[R_KT], disjoint=True)
                        if c0 + n > 1024:
                            lo = max(c0, 1024)
                            T.op(ACT, lambda e: e.copy(out=KTf[:, lo - 1024:c0 + n - 1024], in_=bank[:, lo - c0:n]),
                                 reads=[R_bank], writes=[R_KTf], disjoint=True)
                    proj([(cg(tA), KT)], 1, xT, R_xT, CH_ALL, evac_k, pb, R_pb)
                    def k_out_tr():
                        tb = state["alt"] % 2
                        state["alt"] += 1
                        fns = []
                        for i in range(5):
                            w = 128 if i < 4 else 32
                            fns.append(lambda e, i=i, w=w, tb=tb: e.transpose(
                                tpf[tb][0:w, (i % 4) * 128:(i % 4) * 128 + 128], KTf[:, i * 128:i * 128 + w], ident_f[:, :]))
                        T.group(fns[0:4], reads=[R_KTf, R_const], writes=[R_tpf[tb]])
                        T.op(ACT, lambda e, tb=tb: e.copy(out=kst[hp][:, 0:4, :],
                                                          in_=tpf[tb][:, :].rearrange("p (j n) -> p j n", n=128)),
                             reads=[R_tpf[tb]], writes=[R_kst[hp]])
                        tb2 = state["alt"] % 2
                        state["alt"] += 1
                        fns5 = [lambda e, tb2=tb2: e.transpose(tpf[tb2][0:32, 0:128], KTf[:, 512:544], ident_f[:, :])]
                        T.group(fns5, reads=[R_KTf, R_const], writes=[R_tpf[tb2]])
                        T.op(ACT, lambda e, tb2=tb2: e.copy(out=kst[hp][0:32, 4, :], in_=tpf[tb2][0:32, 0:128]),
                             reads=[R_tpf[tb2]], writes=[R_kst[hp]], disjoint=True)
                        T.dma(SP, k_o_main[:, :, h * 128:(h + 1) * 128], kst[hp][:, 0:4, :], R_kst[hp], reads=[R_kst[hp]])
                        T.dma(SP, k_o[512:544, h * 128:(h + 1) * 128], kst[hp][0:32, 4, :], R_kst[hp], reads=[R_kst[hp]])

                    def evac_v(ci, bank, R_bank, c0, n):
                        T.op(ACT, lambda e: e.copy(out=VTf[:, c0:c0 + n], in_=bank[:, 0:n]),
                             reads=[R_bank], writes=[R_VTf], disjoint=True)
                    proj([(ws.next, KT)], 0, xT, R_xT, CH_ALL, evac_v, pb, R_pb)
                    k_out_tr()
                    for s_ in range(2):
                        tb = state["alt"] % 2
                        state["alt"] += 1
                        T.group([lambda e, j=j, tb=tb: e.transpose(tpf[tb][:, j * 128:(j + 1) * 128], ckfs[s_][:, j, :],
                                                                 ident_f[:, :]) for j in range(4)],
                                reads=[R_ckfs[s_], R_const], writes=[R_tpf[tb]])
                        T.op(ACT, lambda e, tb=tb: e.copy(out=ckTs[s_][:, :], in_=tpf[tb][:, :]),
                             reads=[R_tpf[tb]], writes=[R_ckTs[s_]])

                    for g4 in range(4):
                        tiles = list(range(g4 * 4, min(g4 * 4 + 4, 13)))
                        tb = state["alt"] % 2
                        state["alt"] += 1
                        fns = []
                        for j, r in enumerate(tiles):
                            w = 128 if r < 12 else 32
                            fns.append(lambda e, j=j, r=r, w=w, tb=tb: e.transpose(
                                tpf[tb][0:w, j * 128:(j + 1) * 128], VTf[:, r * 128:r * 128 + w], ident_f[:, :]))
                        T.group(fns, reads=[R_VTf, R_const], writes=[R_tpf[tb]])
                        if g4 < 3:
                            T.op(DVE, lambda e, tb=tb, g4=g4: e.tensor_copy(
                                out=Vb[:, g4 * 4:g4 * 4 + 4, :], in_=tpf[tb][:, :].rearrange("p (j n) -> p j n", n=128)),
                                reads=[R_tpf[tb]], writes=[R_Vb], disjoint=True)
                            if g4 == 2:
                                T.op(ACT, lambda e, tb=tb: e.copy(
                                    out=vst[hp][:, 0:4, :], in_=tpf[tb][:, :].rearrange("p (j n) -> p j n", n=128)),
                                    reads=[R_tpf[tb]], writes=[R_vst[hp]])
                        else:
                            T.op(DVE, lambda e, tb=tb: e.tensor_copy(out=Vb[0:32, 12, :], in_=tpf[tb][0:32, 0:128]),
                                 reads=[R_tpf[tb]], writes=[R_Vb], disjoint=True)
                            T.op(ACT, lambda e, tb=tb: e.copy(out=vst[hp][0:32, 4, :], in_=tpf[tb][0:32, 0:128]),
                                 reads=[R_tpf[tb]], writes=[R_vst[hp]], disjoint=True)
                    T.dma(SP, v_o_main[:, :, h * 128:(h + 1) * 128], vst[hp][:, 0:4, :], R_vst[hp], reads=[R_vst[hp]])
                    T.dma(SP, v_o[512:544, h * 128:(h + 1) * 128], vst[hp][0:32, 4, :], R_vst[hp], reads=[R_vst[hp]])
                    tb = state["alt"] % 2
                    state["alt"] += 1
                    T.group([lambda e, s=s, tb=tb: e.transpose(tpf[tb][0:16, s * 128:(s + 1) * 128],
                                                             VTf[:, 1536 + 16 * s:1552 + 16 * s], ident_f[:, :])
                             for s in range(2)], reads=[R_VTf, R_const], writes=[R_tpf[tb]])
                    T.op(DVE, lambda e, tb=tb: e.tensor_copy(
                        out=Vs[:, :, :], in_=tpf[tb][0:16, 0:256].rearrange("p (j n) -> p j n", n=128)),
                        reads=[R_tpf[tb]], writes=[R_Vs])

                    def att_S(t):
                        q_ap = QT[:, t * 128:(t + 1) * 128]
                        T.group([lambda e, kb=kb: e.matmul(SAb[:, kb * 128:(kb + 1) * 128],
                                                           KTb[:, (t + kb) * 128:(t + kb + 1) * 128], q_ap,
                                                           start=True, stop=True) for kb in range(4)],
                                reads=[R_KT, R_QT], writes=[R_SA])
                        T.group([lambda e: e.matmul(SBb[:, 0:128], KTb[:, (t + 4) * 128:(t + 5) * 128], q_ap,
                                                    start=True, stop=True)],
                                reads=[R_KT, R_QT], writes=[R_SB])

                    def att_soft(t):
                        tp_ = t % 2
                        nh = max(0, 4 - t)
                        if nh > 0:
                            T.op(DVE, lambda e: e.scalar_tensor_tensor(
                                out=tmp[tp_][:, 0:nh * 128], in0=SAb[:, 0:nh * 128], scalar=SCALE,
                                in1=bh[:, 0:nh * 128], op0=ALU.mult, op1=ALU.add),
                                reads=[R_SA, R_bh], writes=[R_tmp[tp_]])
                        if nh < 4:
                            T.op(DVE, lambda e: e.scalar_tensor_tensor(
                                out=tmp[tp_][:, nh * 128:512], in0=SAb[:, nh * 128:512], scalar=SCALE,
                                in1=bm[:, nh * 128:512], op0=ALU.mult, op1=ALU.add),
                                reads=[R_SA, R_bm], writes=[R_tmp[tp_]], disjoint=(nh > 0))
                        T.op(DVE, lambda e: e.scalar_tensor_tensor(
                            out=tmp[tp_][:, 512:640], in0=SBb[:, 0:128], scalar=SCALE,
                            in1=bm[:, 512:640], op0=ALU.mult, op1=ALU.add),
                            reads=[R_SB, R_bm], writes=[R_tmp[tp_]], disjoint=True)
                        T.op(ACT, lambda e: e.activation(out=PT[tp_][:, :], in_=tmp[tp_][:, :], func=AF.Exp),
                             reads=[R_tmp[tp_]], writes=[R_PT[tp_]])

                    def att_PV(t):
                        tp_ = t % 2
                        Ob, R_O, rsb, R_rsb = Obs[tp_], R_Os[tp_], rsbs[tp_], R_rsbs[tp_]
                        T.group([lambda e, kb=kb: e.matmul(Ob[:, 0:128], Vb[:, t + kb, :],
                                                           PT[tp_][:, kb * 128:(kb + 1) * 128],
                                                           start=(kb == 0), stop=(kb == 4)) for kb in range(5)] +
                                [lambda e, kb=kb: e.matmul(Ob[:, 128:256], ones_b[:, :],
                                                           PT[tp_][:, kb * 128:(kb + 1) * 128],
                                                           start=(kb == 0), stop=(kb == 4), skip_group_check=True)
                                 for kb in range(5)],
                                reads=[R_Vb, R_PT[tp_], R_const], writes=[R_O])
                        T.op(DVE, lambda e: e.reciprocal(out=rsb[:, :], in_=Ob[:, 128:256]),
                             reads=[R_O], writes=[R_rsb])
                        T.op(DVE, lambda e: e.tensor_tensor(out=atT[hp][:, t * 128:(t + 1) * 128], in0=Ob[:, 0:128],
                                                            in1=rsb[:, :], op=ALU.mult),
                             reads=[R_O, R_rsb], writes=[R_atT[hp]], disjoint=(t > 0))

                    att_S(0)
                    att_soft(0)
                    for t in range(8):
                        if t + 1 < 8:
                            att_S(t + 1)
                            att_soft(t + 1)
                        att_PV(t)

                    bm3 = bm[:, :].rearrange("p (b n) -> p b n", n=128)

                    def sm_S(s_):
                        qs = QT[:, 1024 + 16 * s_:1040 + 16 * s_]
                        o = s_ * 128
                        T.group([lambda e, kb=kb: e.matmul(SAb[:, o + kb * 16:o + (kb + 1) * 16],
                                                           ckTs[s_][:, kb * 128:(kb + 1) * 128], qs,
                                                           start=True, stop=True) for kb in range(4)] +
                                [lambda e: e.matmul(SAb[0:16, o + 64:o + 80], KTb[:, 1536 + 16 * s_:1552 + 16 * s_], qs,
                                                    start=True, stop=True, skip_group_check=True)],
                                reads=[R_ckTs[s_], R_QT, R_KT], writes=[R_SA])
                        tmps, R_tmps, PTs, R_PTs = tmpss[s_], R_tmpss[s_], PTss[s_], R_PTss[s_]
                        T.op(DVE, lambda e: e.scalar_tensor_tensor(
                            out=tmps[:, 0:4, :], in0=SAb[:, o:o + 64].rearrange("p (b n) -> p b n", n=16), scalar=SCALE,
                            in1=bm3[:, 0:4, 0:16], op0=ALU.mult, op1=ALU.add),
                            reads=[R_SA, R_bm], writes=[R_tmps])
                        T.op(DVE, lambda e: e.scalar_tensor_tensor(
                            out=tmps[0:16, 4, :], in0=SAb[0:16, o + 64:o + 80], scalar=SCALE,
                            in1=bm[0:16, 512:528], op0=ALU.mult, op1=ALU.add),
                            reads=[R_SA, R_bm], writes=[R_tmps], disjoint=True)
                        T.op(ACT, lambda e: e.activation(out=PTs[:, 0:4, :], in_=tmps[:, 0:4, :], func=AF.Exp),
                             reads=[R_tmps], writes=[R_PTs])
                        T.op(ACT, lambda e: e.activation(out=PTs[0:16, 4, :], in_=tmps[0:16, 4, :], func=AF.Exp),
                             reads=[R_tmps], writes=[R_PTs], disjoint=True)

                    def sm_PV(s_):
                        Ob, R_O, rsb, R_rsb = Obs[s_], R_Os[s_], rsbs[s_], R_rsbs[s_]
                        PTs, R_PTs, cvb, R_cvb = PTss[s_], R_PTss[s_], cvbs[s_], R_cvbs[s_]
                        T.group([lambda e, kb=kb: e.matmul(Ob[:, 0:16], cvb[:, kb, :], PTs[:, kb, :],
                                                           start=(kb == 0), stop=False) for kb in range(4)] +
                                [lambda e: e.matmul(Ob[:, 0:16], Vs[0:16, s_, :], PTs[0:16, 4, :],
                                                    start=False, stop=True)] +
                                [lambda e, kb=kb: e.matmul(Ob[:, 128:144], ones_b[:, :], PTs[:, kb, :],
                                                           start=(kb == 0), stop=False, skip_group_check=True)
                                 for kb in range(4)] +
                                [lambda e: e.matmul(Ob[:, 128:144], ones_b[0:16, :], PTs[0:16, 4, :],
                                                    start=False, stop=True, skip_group_check=True)],
                                reads=[R_cvb, R_PTs, R_Vs, R_const], writes=[R_O])
                        T.op(DVE, lambda e: e.reciprocal(out=rsb[:, 0:16], in_=Ob[:, 128:144]),
                             reads=[R_O], writes=[R_rsb])
                        T.op(DVE, lambda e: e.tensor_tensor(out=atT[hp][:, 1024 + 16 * s_:1040 + 16 * s_],
                                                            in0=Ob[:, 0:16], in1=rsb[:, 0:16], op=ALU.mult),
                             reads=[R_O, R_rsb], writes=[R_atT[hp]], disjoint=True)

                    sm_S(0)
                    sm_S(1)
                    sm_PV(0)
                    sm_PV(1)
                    T.dma(SP, mix_scr[h], atT[hp][:, :], R_atT[hp], reads=[R_atT[hp]])
                T.barrier()
                if KSTOP == 2:
                    T.final()
                    return nc
                T.release(*R_kst, *R_vst, *R_atT, R_braw, R_mk, *R_ckfs, *R_cvbs)

            with ExitStack() as S3:
                pb = [ps(S3, f"pj{i}", [128, 512]) for i in range(4)]
                R_pb = [T.res(f"psum:pj{i}") for i in range(4)]
                tpf = [ps(S3, f"tpf{i}", [128, 512]) for i in range(2)]
                R_tpf = [T.res(f"psum:tpf{i}") for i in range(2)]
                uT = sb(S3, "uT", [128, NALL], F32)
                pa = sb(S3, "pa", [128, NALL], F32)
                pbuf = sb(S3, "pbuf", [128, NALL], F32)
                us = sb(S3, "us", [128, 2, 32], F32)
                qa = sb(S3, "qa", [128, 2, 32], F32)
                qb = sb(S3, "qb", [128, 2, 32], F32)
                dfix = sb(S3, "dfix", [128, 16], F32)
                dT = sb(S3, "dT", [128, 4, NM], BF16)
                poT = [sb(S3, f"poT{i}", [128, NM], BF16) for i in range(2)]
                spb = sb(S3, "spb", [15, 2, 2048], F32)
                ust = sb(S3, "ust", [48, 2048], F32)
                icn = sb(S3, "icn", [128, 64], F32)
                psc = sb(S3, "psc", [128, 16], F32)
                R_uT, R_pa, R_pbuf, R_us, R_qa, R_qb, R_dfix, R_dT = (
                    T.res(n) for n in ("uT", "pa", "pbuf", "us", "qa", "qb", "dfix", "dT"))
                R_poT = [T.res(f"poT{i}", dma=True) for i in range(2)]
                R_spb = T.res("spb", dma=True)
                R_ust = T.res("ust", dma=True)
                R_icn = T.res("icn", dma=True)
                T.dma(SP, spb[:, :, :], spd.rearrange("s r c -> r s c"), R_spb, writes=[R_spb])
                T.dma(SP, icn[:, :], invcnt[:, :], R_icn, writes=[R_icn])
                T.dma(SP, psc[:, :], pscale[:, :], R_icn, writes=[R_icn], disjoint=True)
                pcount = 0
                for g in range(4):
                    wwin = 2 ** (g + 1)
                    T.epoch(R_dT)
                    for i in range(2):
                        tU = ws.next()
                        for mt in range(2):
                            j = g * 4 + i * 2 + mt
                            T.epoch(R_uT)

                            def evac_u(ci, bank, R_bank, c0, n):
                                T.op(ACT, lambda e: e.copy(out=uT[:, c0:c0 + n], in_=bank[:, 0:n]),
                                     reads=[R_bank], writes=[R_uT], disjoint=True)
                            proj([(cg(tU), KT)], mt, xT, R_xT, CH_U, evac_u, pb, R_pb)
                            tb = state["alt"] % 2
                            state["alt"] += 1
                            T.group([lambda e, s=s, tb=tb: e.transpose(tpf[tb][:, s * 16:s * 16 + 15],
                                                                     spb[0:15, s, j * 128:(j + 1) * 128],
                                                                     ident_f[0:15, 0:15]) for s in range(2)] +
                                    [lambda e, tb=tb: e.transpose(tpf[tb][0:48, 128:256], uT[:, 1520:1568],
                                                                  ident_f[:, :])],
                                    reads=[R_spb, R_const, R_uT], writes=[R_tpf[tb]])
                            T.op(ACT, lambda e, tb=tb: e.copy(
                                out=us[:, :, 0:15], in_=tpf[tb][:, 0:32].rearrange("p (s n) -> p s n", n=16)[:, :, 0:15]),
                                reads=[R_tpf[tb]], writes=[R_us])
                            T.op(ACT, lambda e, tb=tb: e.copy(out=ust[0:48, j * 128:(j + 1) * 128],
                                                              in_=tpf[tb][0:48, 128:256]),
                                 reads=[R_tpf[tb]], writes=[R_ust], disjoint=True)
                            T.op(DVE, lambda e: e.tensor_copy(
                                out=us[:, :, 15:31], in_=uT[:, 1536:1568].rearrange("p (s n) -> p s n", n=16)),
                                reads=[R_uT], writes=[R_us], disjoint=True)
                            srcs = [(uT, R_uT), (pa, R_pa), (pbuf, R_pbuf), (pa, R_pa), (pbuf, R_pbuf)]
                            ssrc = [(us, R_us), (qa, R_qa), (qb, R_qb), (qa, R_qa), (qb, R_qb)]
                            lo, slo = 496, 0
                            for st in range(g + 1):
                                sh = 2 ** st
                                lo += sh
                                slo += sh
                                (a, Ra), (o, Ro) = srcs[st], srcs[st + 1]
                                T.op(DVE, lambda e, a=a, o=o, lo=lo, sh=sh: e.tensor_tensor(
                                    out=o[:, lo:1536], in0=a[:, lo:1536], in1=a[:, lo - sh:1536 - sh], op=ALU.add),
                                    reads=[Ra], writes=[Ro])
                                (a2, Ra2), (o2, Ro2) = ssrc[st], ssrc[st + 1]
                                T.op(DVE, lambda e, a2=a2, o2=o2, slo=slo, sh=sh: e.tensor_tensor(
                                    out=o2[:, :, slo:31], in0=a2[:, :, slo:31], in1=a2[:, :, slo - sh:31 - sh],
                                    op=ALU.add), reads=[Ra2], writes=[Ro2])
                            (wsu, Rws), (wss, Rwss) = srcs[g + 1], ssrc[g + 1]
                            jj = j % 4
                            T.op(DVE, lambda e: e.scalar_tensor_tensor(
                                out=dT[:, jj, 0:1024], in0=wsu[:, 512:1536], scalar=1.0 / wwin,
                                in1=uT[:, 512:1536], op0=ALU.mult, op1=ALU.subtract),
                                reads=[Rws, R_uT], writes=[R_dT], disjoint=True)
                            T.op(DVE, lambda e: e.tensor_tensor(out=dfix[:, :], in0=wsu[:, 512:528],
                                                                in1=icn[:, g * 16:(g + 1) * 16], op=ALU.mult),
                                 reads=[Rws, R_icn], writes=[R_dfix])
                            T.op(DVE, lambda e: e.tensor_tensor(out=dT[:, jj, 0:16], in0=dfix[:, :],
                                                                in1=uT[:, 512:528], op=ALU.subtract),
                                 reads=[R_dfix, R_uT, R_dT], writes=[R_dT], disjoint=True)
                            T.op(DVE, lambda e: e.scalar_tensor_tensor(
                                out=dT[:, jj, 1024:1056].rearrange("p (s n) -> p s n", n=16), in0=wss[:, :, 15:31],
                                scalar=1.0 / wwin, in1=us[:, :, 15:31], op0=ALU.mult, op1=ALU.subtract),
                                reads=[Rwss, R_us], writes=[R_dT], disjoint=True)
                    tP = ws.next()
                    for mt in range(4):
                        pp = pcount % 2
                        pcount += 1
                        T.epoch(R_poT[pp])

                        def evac_p(ci, bank, R_bank, c0, n):
                            T.op(DVE, lambda e: e.tensor_scalar(
                                out=poT[pp][:, c0:c0 + n], in0=bank[:, 0:n],
                                scalar1=psc[:, g * 4 + mt:g * 4 + mt + 1], scalar2=None, op0=ALU.mult),
                                reads=[R_bank, R_icn], writes=[R_poT[pp]], disjoint=True)
                        proj([(cg(tP), 4)], mt, dT, R_dT, CH_MAIN, evac_p, pb, R_pb)
                        T.dma(SP, mix_scr[16 + g * 4 + mt], poT[pp][:, :], R_poT[pp], reads=[R_poT[pp]])
                T.dma(SP, u_o[:, :], ust[:, :], R_ust, reads=[R_ust])
                T.barrier()
                if KSTOP == 3:
                    T.final()
                    return nc
                T.release(*R_poT, R_spb, R_ust, R_icn)

        def project_to_tokmajor(stack, n_slabs_m, get_segs, inT, R_in, ntok, chunks, scr, row0, hook=None):
            pb = [ps(stack, f"pj{i}", [128, 512]) for i in range(4)]
            R_pb = [T.res(f"psum:pj{i}") for i in range(4)]
            tpf = [ps(stack, f"tpf{i}", [128, 512]) for i in range(3)]
            R_tpf = [T.res(f"psum:tpf{i}") for i in range(3)]
            zst = [sb(stack, f"zst{i}", [128, ntok], F32) for i in range(2)]
            R_zst = [T.res(f"zst{i}") for i in range(2)]
            ntile = (ntok + 127) // 128
            ost = [sb(stack, f"ost{i}", [128, ntile, 128], F32) for i in range(2)]
            R_ost = [T.res(f"ost{i}", dma=True) for i in range(2)]
            nfull = ntok // 128
            rem = ntok - nfull * 128
            scr_main = scr[row0:row0 + nfull * 128, :].rearrange("(t p) c -> p t c", p=128)
            tcount = 0
            for m in range(n_slabs_m):
                if hook is not None:
                    hook(m)
                segs, mt = get_segs(m)
                zp = m % 2
                T.epoch(R_zst[zp])

                def evac_z(ci, bank, R_bank, c0, n):
                    T.op(ACT, lambda e: e.copy(out=zst[zp][:, c0:c0 + n], in_=bank[:, 0:n]),
                         reads=[R_bank], writes=[R_zst[zp]], disjoint=True)
                proj(segs, mt, inT, R_in, chunks, evac_z, pb, R_pb)
                T.epoch(R_ost[zp])
                for g4 in range((ntile + 3) // 4):
                    tiles = list(range(g4 * 4, min(g4 * 4 + 4, ntile)))
                    full = [r for r in tiles if r < nfull]
                    part = [r for r in tiles if r >= nfull]
                    tb = tcount % 3
                    tcount += 1
                    fns = []
                    for j, r in enumerate(tiles):
                        w = 128 if r < nfull else rem
                        fns.append(lambda e, j=j, r=r, w=w, tb=tb: e.transpose(
                            tpf[tb][0:w, j * 128:(j + 1) * 128], zst[zp][:, r * 128:r * 128 + w], ident_f[:, :]))
                    T.group(fns, reads=[R_zst[zp], R_const], writes=[R_tpf[tb]])
                    if full:
                        nf = len(full)
                        T.op(DVE, lambda e, tb=tb, nf=nf, f0=full[0]: e.tensor_copy(
                            out=ost[zp][:, f0:f0 + nf, :],
                            in_=tpf[tb][:, 0:nf * 128].rearrange("p (j n) -> p j n", n=128)),
                            reads=[R_tpf[tb]], writes=[R_ost[zp]], disjoint=True)
                    if part:
                        j = len(full)
                        T.op(DVE, lambda e, tb=tb, j=j, r=part[0]: e.tensor_copy(
                            out=ost[zp][0:rem, r, :], in_=tpf[tb][0:rem, j * 128:(j + 1) * 128]),
                            reads=[R_tpf[tb]], writes=[R_ost[zp]], disjoint=True)
                T.dma(SP, scr_main[:, :, m * 128:(m + 1) * 128], ost[zp][:, 0:nfull, :], R_ost[zp], reads=[R_ost[zp]])
                if rem:
                    T.dma(SP, scr[row0 + nfull * 128:row0 + ntok, m * 128:(m + 1) * 128], ost[zp][0:rem, nfull, :],
                          R_ost[zp], reads=[R_ost[zp]])
            return R_ost

        with ExitStack() as S4:
            mixT = sb(S4, "mixT", [128, KT, NM], BF16)
            R_mix = T.res("mixT", dma=True)
            for q4 in range(4):
                T.dma(SP, mixT[:, q4 * 8:(q4 + 1) * 8, :], mix_scr[q4 * 8:(q4 + 1) * 8].rearrange("j p n -> p j n"),
                      R_mix, writes=[R_mix], disjoint=(q4 > 0))
            cur = {}

            def segs_out(m):
                if m % 2 == 0:
                    cur["s"] = ws.next()
                return [(cg(cur["s"]), KT)], m % 2
            R_o = project_to_tokmajor(S4, 32, segs_out, mixT, R_mix, NM, CH_MAIN, z1_scr, 0)
            T.barrier()
            if KSTOP == 4:
                T.final()
                return nc
            T.release(R_mix, *R_o)

        def layer_norm_pass(stack, zsrc, xsrc, xrow0, g_in, b_in, out_dram, after_tile, tiles=tuple(range(9))):
            zts = [sb(stack, f"zt{i}", [128, D], F32) for i in range(2)]
            xt = sb(stack, "xt", [128, D], F32)
            gt = sb(stack, "gt", [128, D], F32)
            bt = sb(stack, "bt", [128, D], F32)
            sts = [sb(stack, f"st{i}", [128, 64], F32) for i in range(2)]
            R_zts = [T.res(f"zt{i}", dma=True) for i in range(2)]
            R_xt, R_gb = T.res("xt", dma=True), T.res("gb", dma=True)
            R_sts = [T.res(f"st{i}") for i in range(2)]
            T.dma(SP, gt[:, :], g_in[:, :], R_gb, writes=[R_gb])
            T.dma(SP, bt[:, :], b_in[:, :], R_gb, writes=[R_gb], disjoint=True)
            def geom(r):
                return (128 if r < 8 else 32), r * 128

            def load_z(r):
                rows, r0 = geom(r)
                T.dma(SP, zts[r % 2][0:rows, :], zsrc[r0:r0 + rows, :], R_zts[r % 2], writes=[R_zts[r % 2]])

            def load_x(r):
                rows, r0 = geom(r)
                T.dma(SP, xt[0:rows, :], xsrc[xrow0 + r0:xrow0 + r0 + rows, :], R_xt, writes=[R_xt])

            load_z(tiles[0])
            load_x(tiles[0])
            for ti, r in enumerate(tiles):
                rows, r0 = geom(r)
                zt, R_zt, st, R_st = zts[r % 2], R_zts[r % 2], sts[r % 2], R_sts[r % 2]
                if ti + 1 < len(tiles):
                    load_z(tiles[ti + 1])
                T.op(DVE, lambda e: e.scalar_tensor_tensor(out=zt[0:rows, :], in0=xt[0:rows, :], scalar=ALPHA,
                                                           in1=zt[0:rows, :], op0=ALU.mult, op1=ALU.add),
                     reads=[R_xt, R_zt], writes=[R_zt])
                if ti + 1 < len(tiles):
                    load_x(tiles[ti + 1])
                for c in range(8):
                    T.op(DVE, lambda e, c=c: e.bn_stats(st[0:rows, c * 6:(c + 1) * 6], zt[0:rows, c * 512:(c + 1) * 512]),
                         reads=[R_zt], writes=[R_st], disjoint=(c > 0))
                T.op(DVE, lambda e: e.bn_aggr(st[0:rows, 48:50], st[0:rows, 0:48]), reads=[R_st], writes=[R_st])
                T.op(DVE, lambda e: e.tensor_scalar(out=st[0:rows, 50:51], in0=st[0:rows, 49:50], scalar1=EPS,
                                                    scalar2=None, op0=ALU.add), reads=[R_st], writes=[R_st])
                T.op(ACT, lambda e: e.activation(out=st[0:rows, 51:52], in_=st[0:rows, 50:51], func=AF.Sqrt),
                     reads=[R_st], writes=[R_st])
                T.op(DVE, lambda e: e.reciprocal(out=st[0:rows, 52:53], in_=st[0:rows, 51:52]),
                     reads=[R_st], writes=[R_st])
                T.op(DVE, lambda e: e.scalar_tensor_tensor(out=st[0:rows, 53:54], in0=st[0:rows, 48:49], scalar=-1.0,
                                                           in1=st[0:rows, 52:53], op0=ALU.mult, op1=ALU.mult),
                     reads=[R_st], writes=[R_st])
                T.op(ACT, lambda e: e.activation(out=zt[0:rows, :], in_=zt[0:rows, :], func=AF.Identity,
                                                 bias=st[0:rows, 53:54], scale=st[0:rows, 52:53]),
                     reads=[R_st, R_zt], writes=[R_zt])
                T.op(DVE, lambda e: e.tensor_tensor(out=zt[0:rows, :], in0=zt[0:rows, :], in1=gt[0:rows, :],
                                                    op=ALU.mult), reads=[R_zt, R_gb], writes=[R_zt])
                T.op(DVE, lambda e: e.tensor_tensor(out=zt[0:rows, :], in0=zt[0:rows, :], in1=bt[0:rows, :],
                                                    op=ALU.add), reads=[R_zt, R_gb], writes=[R_zt])
                T.dma(SP, out_dram[r0:r0 + rows, :], zt[0:rows, :], R_zt, reads=[R_zt])
                if after_tile is not None:
                    after_tile(r, rows, r0, zt, R_zt)
            return [*R_zts, R_xt, R_gb]

        with ExitStack() as SB_:
            x1T = sb(SB_, "x1T", [128, KT, NM], BF16)
            R_x1T = T.res("x1T")
            with ExitStack() as S5:
                tpf = [ps(S5, f"tpf{i}", [128, 512]) for i in range(6)]
                R_tpf = [T.res(f"psum:tpf{i}") for i in range(6)]
                cc = {"n": 0}

                def after1(r, rows, r0, zt, R_zt):
                    for g8 in range(8):
                        b = cc["n"] % 6
                        cc["n"] += 1
                        T.group([lambda e, j=j, b=b: e.transpose(
                            tpf[b][:, j * 128:j * 128 + rows], zt[0:rows, (g8 * 4 + j) * 128:(g8 * 4 + j + 1) * 128],
                            ident_f[0:rows, 0:rows]) for j in range(4)],
                            reads=[R_zt, R_const], writes=[R_tpf[b]])
                        T.op(ACT, lambda e, b=b: e.copy(
                            out=x1T[:, g8 * 4:(g8 + 1) * 4, r0:r0 + rows],
                            in_=tpf[b][:, :].rearrange("p (j n) -> p j n", n=128)[:, :, 0:rows]),
                            reads=[R_tpf[b]], writes=[R_x1T], disjoint=True)
                rel = layer_norm_pass(S5, z1_scr, xc, NH, g1, b1, x1_scr, after1)
                T.barrier()
                if KSTOP == 5:
                    T.final()
                    return nc
                T.release(*rel)

            with ExitStack() as S6:
                pb = [ps(S6, f"pj{i}", [128, 512]) for i in range(6)]
                R_pb = [T.res(f"psum:pj{i}") for i in range(6)]
                sg = [sb(S6, f"sg{i}", [128, 352], F32) for i in range(2)]
                R_sg = [T.res(f"sg{i}") for i in range(2)]
                hst = [sb(S6, f"hst{i}", [128, NM], BF16) for i in range(2)]
                R_hst = [T.res(f"hst{i}", dma=True) for i in range(2)]
                n_sg = 0
                for j in range(KTF):
                    slabGU, R_GU = ws.next()
                    hp = j % 2
                    T.epoch(R_hst[hp])
                    for ci, (c0, n) in enumerate(CH_MAIN):
                        bg = state["bank"] % 6
                        bu = (state["bank"] + 1) % 6
                        state["bank"] += 2
                        T.group([lambda e, k=k: e.matmul(pb[bg][:, 0:n], slabGU[:, k, 0:128],
                                                         x1T[:, k, c0:c0 + n], start=(k == 0),
                                                         stop=(k == KT - 1)) for k in range(KT)],
                                reads=[R_GU, R_x1T], writes=[R_pb[bg]])
                        T.group([lambda e, k=k: e.matmul(pb[bu][:, 0:n], slabGU[:, k, 128:256],
                                                         x1T[:, k, c0:c0 + n], start=(k == 0),
                                                         stop=(k == KT - 1)) for k in range(KT)],
                                reads=[R_GU, R_x1T], writes=[R_pb[bu]])
                        sp_ = n_sg % 2
                        n_sg += 1
                        T.op(ACT, lambda e: e.activation(out=sg[sp_][:, 0:n], in_=pb[bg][:, 0:n], func=AF.Silu),
                             reads=[R_pb[bg]], writes=[R_sg[sp_]])
                        T.op(DVE, lambda e: e.tensor_tensor(
                            out=hst[hp][:, c0:c0 + n], in0=pb[bu][:, 0:n], in1=sg[sp_][:, 0:n], op=ALU.mult),
                            reads=[R_pb[bu], R_sg[sp_]], writes=[R_hst[hp]], disjoint=True)
                    T.dma(SP, hT_scr[j], hst[hp][:, :], R_hst[hp], reads=[R_hst[hp]])
                T.barrier()
                if KSTOP == 6:
                    T.final()
                    return nc
                T.release(*R_hst)

        for hf in range(2):
            with ExitStack() as S7:
                hT = sb(S7, "hT", [128, KTF, 528], BF16)
                R_hTs = [T.res("hTlo", dma=True), T.res("hThi", dma=True)]
                for part, (ka, kb_) in enumerate(((0, 43), (43, KTF))):
                    for q4 in range(ka, kb_, 8):
                        q5 = min(q4 + 8, kb_)
                        T.dma(SP, hT[:, q4:q5, :],
                              hT_scr[q4:q5, :, hf * 528:(hf + 1) * 528].rearrange("j p n -> p j n"),
                              R_hTs[part], writes=[R_hTs[part]], disjoint=(q4 > ka))

                def segs_down(m):
                    return [(ws.next, 43), (ws.next, 43)], 0
                hook = None
                rel_ln = []
                if hf == 1:
                    lzt = sb(S7, "lzt", [128, D], F32)
                    lxt = sb(S7, "lxt", [128, D], F32)
                    lgh = sb(S7, "lgh", [128, 2048], F32)
                    lbh = sb(S7, "lbh", [128, 2048], F32)
                    lst = sb(S7, "lst", [128, 64], F32)
                    R_lzt, R_lxt, R_lgh = T.res("lzt", dma=True), T.res("lxt", dma=True), T.res("lgh", dma=True)
                    R_lst = T.res("lst")
                    rel_ln = [R_lzt, R_lxt, R_lgh]

                    def hook(m):
                        i, ph = m // 8, m % 8
                        r0 = i * 128
                        if ph == 0:
                            T.dma(SP, lzt[:, :], z2_scr[r0:r0 + 128, :], R_lzt, writes=[R_lzt])
                            T.dma(SP, lxt[:, :], x1_scr[r0:r0 + 128, :], R_lxt, writes=[R_lxt])
                            T.dma(SP, lgh[:, :], g2[:, 0:2048], R_lgh, writes=[R_lgh])
                            T.dma(SP, lbh[:, :], b2[:, 0:2048], R_lgh, writes=[R_lgh], disjoint=True)
                        elif ph == 1:
                            T.op(DVE, lambda e: e.scalar_tensor_tensor(out=lzt[:, :], in0=lxt[:, :], scalar=ALPHA,
                                                                       in1=lzt[:, :], op0=ALU.mult, op1=ALU.add),
                                 reads=[R_lxt, R_lzt], writes=[R_lzt])
                            for c in range(8):
                                T.op(DVE, lambda e, c=c: e.bn_stats(lst[:, c * 6:(c + 1) * 6], lzt[:, c * 512:(c + 1) * 512]),
                                     reads=[R_lzt], writes=[R_lst], disjoint=(c > 0))
                            T.op(DVE, lambda e: e.bn_aggr(lst[:, 48:50], lst[:, 0:48]), reads=[R_lst], writes=[R_lst])
                            T.op(DVE, lambda e: e.tensor_scalar(out=lst[:, 50:51], in0=lst[:, 49:50], scalar1=EPS,
                                                                scalar2=None, op0=ALU.add), reads=[R_lst], writes=[R_lst])
                            T.op(ACT, lambda e: e.activation(out=lst[:, 51:52], in_=lst[:, 50:51], func=AF.Sqrt),
                                 reads=[R_lst], writes=[R_lst])
                            T.op(DVE, lambda e: e.reciprocal(out=lst[:, 52:53], in_=lst[:, 51:52]),
                                 reads=[R_lst], writes=[R_lst])
                            T.op(DVE, lambda e: e.scalar_tensor_tensor(out=lst[:, 53:54], in0=lst[:, 48:49], scalar=-1.0,
                                                                       in1=lst[:, 52:53], op0=ALU.mult, op1=ALU.mult),
                                 reads=[R_lst], writes=[R_lst])
                        elif ph == 2:
                            T.op(ACT, lambda e: e.activation(out=lzt[:, :], in_=lzt[:, :], func=AF.Identity,
                                                             bias=lst[:, 53:54], scale=lst[:, 52:53]),
                                 reads=[R_lst, R_lzt], writes=[R_lzt])
                            T.op(DVE, lambda e: e.tensor_tensor(out=lzt[:, 0:2048], in0=lzt[:, 0:2048], in1=lgh[:, :],
                                                                op=ALU.mult), reads=[R_lzt, R_lgh], writes=[R_lzt])
                            T.op(DVE, lambda e: e.tensor_tensor(out=lzt[:, 0:2048], in0=lzt[:, 0:2048], in1=lbh[:, :],
                                                                op=ALU.add), reads=[R_lzt, R_lgh], writes=[R_lzt])
                        elif ph == 3:
                            T.dma(SP, lxt[:, 0:2048], g2[:, 2048:4096], R_lxt, writes=[R_lxt])
                            T.dma(SP, lxt[:, 2048:4096], b2[:, 2048:4096], R_lxt, writes=[R_lxt], disjoint=True)
                        elif ph == 5:
                            T.op(DVE, lambda e: e.tensor_tensor(out=lzt[:, 2048:4096], in0=lzt[:, 2048:4096],
                                                                in1=lxt[:, 0:2048], op=ALU.mult),
                                 reads=[R_lzt, R_lxt], writes=[R_lzt])
                            T.op(DVE, lambda e: e.tensor_tensor(out=lzt[:, 2048:4096], in0=lzt[:, 2048:4096],
                                                                in1=lxt[:, 2048:4096], op=ALU.add),
                                 reads=[R_lzt, R_lxt], writes=[R_lzt])
                        elif ph == 7:
                            T.dma(SP, y_o[r0:r0 + 128, :], lzt[:, :], R_lzt, reads=[R_lzt])
                R_o = project_to_tokmajor(S7, 32, segs_down, hT, R_hTs, 528, [(0, 264), (264, 264)], z2_scr, hf * 528,
                                          hook=hook)
                T.barrier()
                if KSTOP == 7:
                    T.final()
                    return nc
                T.release(*R_hTs, *R_o, *rel_ln)

        with ExitStack() as S8:
            rel = layer_norm_pass(S8, z2_scr, x1_scr, 0, g2, b2, y_o, None, tiles=(4, 5, 6, 7, 8))
            T.barrier()
            if KSTOP == 8:
                T.final()
                return nc
            T.release(*rel)
        T.final()
    return nc


_NC_CACHE = {}


def _host_tables():
    j = np.arange(128)[:, None, None]
    kb = np.arange(5)[None, :, None]
    i = np.arange(128)[None, None, :]
    dist = 512 - 128 * kb + i - j
    idx = np.clip(dist, -256, 256) + 256
    kk = kb * 128 + j
    vis = np.where(i < 64, kk < 576, kk >= 64)
    maskT = np.where(vis, 0.0, NEG).astype(np.float32).reshape(128, 640)
    return idx.reshape(128, 640), maskT


def kernel(x_prompt, x_sample, cache_k, cache_v, state_pool, w_in, rel_bias, w_pool, pool_scale,
           w_out, ln1_g, ln1_b, w_gate, w_up, w_down, ln2_g, ln2_b):
    f = lambda a: np.ascontiguousarray(np.asarray(a, dtype=np.float32))
    x_prompt, x_sample, cache_k, cache_v, state_pool = map(f, (x_prompt, x_sample, cache_k, cache_v, state_pool))
    if "nc" not in _NC_CACHE:
        _NC_CACHE["nc"] = build_nc()
    nc = _NC_CACHE["nc"]
    idx, maskT = _host_tables()
    biasT = f(np.asarray(rel_bias, np.float32)[0][:, idx])
    common = {
        "w_in": f(w_in[0]), "w_out": f(w_out[0]), "w_gate": f(w_gate[0]), "w_up": f(w_up[0]), "w_down": f(w_down[0]),
        "w_pool": f(w_pool[0]),
        "pscale": f(np.asarray(pool_scale, np.float32)[0].reshape(16, 128).T),
        "g1": f(np.broadcast_to(np.asarray(ln1_g, np.float32)[0][None, :], (128, D))),
        "b1": f(np.broadcast_to(np.asarray(ln1_b, np.float32)[0][None, :], (128, D))),
        "g2": f(np.broadcast_to(np.asarray(ln2_g, np.float32)[0][None, :], (128, D))),
        "b2": f(np.broadcast_to(np.asarray(ln2_b, np.float32)[0][None, :], (128, D))),
        "biasT": biasT, "maskT": maskT, "ident": np.eye(128, dtype=np.float32),
    }
    wins = np.array([2, 4, 8, 16])
    in_maps = []
    for c in range(NCORES):
        b, half = c // 2, c % 2
        xc = np.zeros((NALL, D), np.float32)
        if half == 1:
            xc[0:NH] = x_prompt[b, 512:1024]
        xc[NH:NH + NP] = x_prompt[b, half * 1024:(half + 1) * 1024]
        xc[NH + NP:NH + NP + 16] = x_sample[2 * c]
        xc[NH + NP + 16:NALL] = x_sample[2 * c + 1]
        t = np.arange(16)
        if half == 0:
            cntv = np.minimum(wins[:, None], t[None, :] + 1).astype(np.float32)
        else:
            cntv = np.broadcast_to(wins[:, None].astype(np.float32), (4, 16))
        inv = (1.0 / cntv).astype(np.float32).reshape(1, 64)
        m = dict(common)
        m.update({
            "xc": xc,
            "ck": f(cache_k[0, 2 * c:2 * c + 2].reshape(2, 512, 2048)),
            "cv": f(cache_v[0, 2 * c:2 * c + 2].reshape(2, 512, 2048)),
            "sp": f(state_pool[0, 2 * c:2 * c + 2]),
            "halo": np.full((128, 1), 0.0 if half == 1 else NEG, np.float32),
            "invcnt": f(np.broadcast_to(inv, (128, 64))),
        })
        in_maps.append(m)
    if _NC_CACHE.get("prep_only"):
        return in_maps
    res = run_bass_kernel_spmd(nc, in_maps, core_ids=list(range(NCORES)))
    R = res.results
    y_prompt = np.zeros((4, 2048, D), np.float32)
    y_sample = np.zeros((16, 16, D), np.float32)
    kp = np.zeros((1, 4, 512, 16, 128), np.float32)
    vp = np.zeros((1, 4, 512, 16, 128), np.float32)
    pp = np.zeros((1, 4, 15, 2048), np.float32)
    ksn = np.zeros((1, 16, 16, 16, 128), np.float32)
    vsn = np.zeros((1, 16, 16, 16, 128), np.float32)
    psn = np.zeros((1, 16, 15, 2048), np.float32)
    for c in range(NCORES):
        b, half = c // 2, c % 2
        y = np.asarray(R[c]["y"])
        ko = np.asarray(R[c]["k_out"])
        vo = np.asarray(R[c]["v_out"])
        uo = np.asarray(R[c]["u_out"])
        y_prompt[b, half * 1024:(half + 1) * 1024] = y[0:1024]
        for s in range(2):
            y_sample[2 * c + s] = y[1024 + 16 * s:1040 + 16 * s]
            ksn[0, 2 * c + s] = ko[512 + 16 * s:528 + 16 * s].reshape(16, 16, 128)
            vsn[0, 2 * c + s] = vo[512 + 16 * s:528 + 16 * s].reshape(16, 16, 128)
            psn[0, 2 * c + s] = uo[16 + 16 * s + 1:16 + 16 * s + 16]
        if half == 1:
            kp[0, b] = ko[0:512].reshape(512, 16, 128)
            vp[0, b] = vo[0:512].reshape(512, 16, 128)
            pp[0, b] = uo[1:16]
    return (y_prompt, y_sample, kp, vp, pp, ksn, vsn, psn)
```
